# Optimizing a Trainium2 kernel written in Bass

```python
import jax, jax.numpy as jnp
from jax import lax
import numpy as np

D_MODEL = 4096
BATCH = 4
SEQ = 4096
DEPTH = 1

HEAD_DIM = 128
A_HEADS = 16
A_KV_HEADS = 4
A_GROUP = A_HEADS // A_KV_HEADS
B_PATTERNS = ((128, 1), (512, 4), (2048, 16))
B_HEADS_PER_GROUP = 8
B_N_GROUPS = len(B_PATTERNS)
B_HEADS = B_N_GROUPS * B_HEADS_PER_GROUP
A_Q_W = A_HEADS * HEAD_DIM
A_KV_W = A_KV_HEADS * HEAD_DIM
A_OUT_W = A_Q_W
B_QKV_W = 3 * B_HEADS * HEAD_DIM
B_OUT_W = B_HEADS_PER_GROUP * HEAD_DIM
GATE_W = 2 * D_MODEL
IN_W = A_Q_W + 2 * A_KV_W + B_QKV_W + GATE_W
D_FF = 11008
GRID_W = 64
ROPE_THETA = 10000.0
ROPE_AXIS_DIM = HEAD_DIM // 2
Q_BLOCK = 128
RMS_EPS = 1e-6
NEG_INF = -1e30

kernel_name = 'hybrid_gated_gqa_dilated_macaron'


def rms_norm(x, g):
    x32 = x.astype(jnp.float32)
    y = x32 * lax.rsqrt(jnp.mean(x32 * x32, axis=-1, keepdims=True) + RMS_EPS)
    return (y * g.astype(jnp.float32)).astype(x.dtype)


def swiglu(x, w_gate, w_up, w_down):
    return (jax.nn.silu(x @ w_gate) * (x @ w_up)) @ w_down


def axial_rope_angles(T):
    rows = T // GRID_W
    row_ids = jnp.repeat(jnp.arange(rows), GRID_W).astype(jnp.float32)
    col_ids = jnp.tile(jnp.arange(GRID_W), rows).astype(jnp.float32)
    inv = ROPE_THETA ** (-jnp.arange(0, ROPE_AXIS_DIM, 2, dtype=jnp.float32) / ROPE_AXIS_DIM)
    return row_ids[:, None] * inv[None, :], col_ids[:, None] * inv[None, :]


def _rotate_half(x, ang):
    half = x.shape[-1] // 2
    x1, x2 = x[..., :half], x[..., half:]
    c = jnp.cos(ang)[:, None, :]
    s = jnp.sin(ang)[:, None, :]
    return jnp.concatenate([x1 * c - x2 * s, x1 * s + x2 * c], axis=-1)


def apply_axial_rope(x, ang_r, ang_c):
    x32 = x.astype(jnp.float32)
    out = jnp.concatenate([_rotate_half(x32[..., :ROPE_AXIS_DIM], ang_r),
                           _rotate_half(x32[..., ROPE_AXIS_DIM:], ang_c)], axis=-1)
    return out.astype(x.dtype)


def global_gqa_attention(q, k, v):
    bsz, T = q.shape[0], q.shape[1]
    nb = T // Q_BLOCK
    qb = q.reshape(bsz, nb, Q_BLOCK, A_KV_HEADS, A_GROUP, HEAD_DIM).transpose(1, 0, 2, 3, 4, 5)
    scale = HEAD_DIM ** -0.5

    def block(qblk):
        s = jnp.einsum('bqkgd,bskd->bkgqs', qblk, k, preferred_element_type=jnp.float32) * scale
        p = jax.nn.softmax(s, axis=-1).astype(v.dtype)
        return jnp.einsum('bkgqs,bskd->bqkgd', p, v)

    out = lax.map(block, qb)
    return out.transpose(1, 0, 2, 3, 4, 5).reshape(bsz, T, A_OUT_W)


def dilated_window_attention(q, k, v, window, dilation, slopes):
    bsz, T, H, hd = q.shape
    half_span = window // 2
    n_side = half_span // dilation
    offsets = jnp.arange(-n_side, n_side + 1) * dilation
    k_pad = jnp.pad(k, ((0, 0), (half_span, half_span), (0, 0), (0, 0)))
    v_pad = jnp.pad(v, ((0, 0), (half_span, half_span), (0, 0), (0, 0)))
    bias = -slopes.astype(jnp.float32)[:, None] * jnp.abs(offsets).astype(jnp.float32)[None, :]
    nb = T // Q_BLOCK
    qb = q.reshape(bsz, nb, Q_BLOCK, H, hd).transpose(1, 0, 2, 3, 4)
    starts = jnp.arange(nb) * Q_BLOCK
    scale = hd ** -0.5

    def block(args):
        qblk, s0 = args
        pos = s0 + jnp.arange(Q_BLOCK)
        key_pos = pos[:, None] + offsets[None, :]
        valid = (key_pos >= 0) & (key_pos < T)
        idx = key_pos + half_span
        kb = jnp.take(k_pad, idx, axis=1)
        vb = jnp.take(v_pad, idx, axis=1)
        s = jnp.einsum('bqhd,bqjhd->bhqj', qblk, kb, preferred_element_type=jnp.float32) * scale
        s = jnp.where(valid[None, None], s + bias[:, None, :], NEG_INF)
        lse = jax.nn.logsumexp(s, axis=-1)
        p = jnp.exp(s - lse[..., None]).astype(v.dtype)
        o = jnp.einsum('bhqj,bqjhd->bqhd', p, vb)
        return o, lse.transpose(0, 2, 1)

    o, lse = lax.map(block, (qb, starts))
    o = o.transpose(1, 0, 2, 3, 4).reshape(bsz, T, H, hd)
    lse = lse.transpose(1, 0, 2, 3).reshape(bsz, T, H)
    return o, lse


def hybrid_layer(h, g_ffn1, w1_gate, w1_up, w1_down, g_mix, w_in, q_norm_a, k_norm_a,
                 w_branch_a, w_branch_b, w_out, g_ffn2, w2_gate, w2_up, w2_down):
    bsz, T, _ = h.shape
    h = h + 0.5 * swiglu(rms_norm(h, g_ffn1), w1_gate, w1_up, w1_down)

    u = rms_norm(h, g_mix)
    proj = u @ w_in
    c0 = A_Q_W
    c1 = c0 + A_KV_W
    c2 = c1 + A_KV_W
    c3 = c2 + B_QKV_W
    q_a = proj[..., :c0].reshape(bsz, T, A_HEADS, HEAD_DIM)
    k_a = proj[..., c0:c1].reshape(bsz, T, A_KV_HEADS, HEAD_DIM)
    v_a = proj[..., c1:c2].reshape(bsz, T, A_KV_HEADS, HEAD_DIM)
    qkv_b = proj[..., c2:c3].reshape(bsz, T, B_N_GROUPS, 3, B_HEADS_PER_GROUP, HEAD_DIM)
    gate_a = proj[..., c3:c3 + D_MODEL]
    gate_b = proj[..., c3 + D_MODEL:]

    ang_r, ang_c = axial_rope_angles(T)
    q_a = apply_axial_rope(rms_norm(q_a, q_norm_a), ang_r, ang_c)
    k_a = apply_axial_rope(rms_norm(k_a, k_norm_a), ang_r, ang_c)
    y_a = global_gqa_attention(q_a, k_a, v_a)

    slopes = jnp.exp2(-8.0 * jnp.arange(1, B_HEADS + 1, dtype=jnp.float32) / B_HEADS)
    outs, lses = [], []
    for gi, (window, dilation) in enumerate(B_PATTERNS):
        o, l = dilated_window_attention(qkv_b[:, :, gi, 0], qkv_b[:, :, gi, 1], qkv_b[:, :, gi, 2],
                                        window, dilation,
                                        slopes[gi * B_HEADS_PER_GROUP:(gi + 1) * B_HEADS_PER_GROUP])
        outs.append(o)
        lses.append(l)
    alpha = jax.nn.softmax(jnp.stack(lses, axis=0), axis=0)
    y_b = jnp.sum(alpha[..., None].astype(outs[0].dtype) * jnp.stack(outs, axis=0), axis=0)
    y_b = y_b.reshape(bsz, T, B_OUT_W)

    merged = jax.nn.sigmoid(gate_a) * (y_a @ w_branch_a) + jax.nn.sigmoid(gate_b) * (y_b @ w_branch_b)
    h = h + merged @ w_out

    h = h + 0.5 * swiglu(rms_norm(h, g_ffn2), w2_gate, w2_up, w2_down)
    return h


def setup_inputs(seed: int = 0) -> dict:
    key = jax.random.key(seed)
    ks = jax.random.split(key, 18)
    L = DEPTH

    def normal(k, shape, fan_in):
        return jax.random.normal(k, shape, jnp.float32) * (fan_in ** -0.5)

    def gain(k, shape):
        return 1.0 + 0.02 * jax.random.normal(k, shape, jnp.float32)

    return {
        'x': jax.random.normal(ks[0], (BATCH, SEQ, D_MODEL), jnp.float32),
        'g_ffn1': gain(ks[1], (L, D_MODEL)),
        'w1_gate': normal(ks[2], (L, D_MODEL, D_FF), D_MODEL),
        'w1_up': normal(ks[3], (L, D_MODEL, D_FF), D_MODEL),
        'w1_down': normal(ks[4], (L, D_FF, D_MODEL), D_FF),
        'g_mix': gain(ks[5], (L, D_MODEL)),
        'w_in': normal(ks[6], (L, D_MODEL, IN_W), D_MODEL),
        'q_norm_a': gain(ks[7], (L, HEAD_DIM)),
        'k_norm_a': gain(ks[8], (L, HEAD_DIM)),
        'w_branch_a': normal(ks[9], (L, A_OUT_W, D_MODEL), A_OUT_W),
        'w_branch_b': normal(ks[10], (L, B_OUT_W, D_MODEL), B_OUT_W),
        'w_out': normal(ks[11], (L, D_MODEL, D_MODEL), D_MODEL),
        'g_ffn2': gain(ks[12], (L, D_MODEL)),
        'w2_gate': normal(ks[13], (L, D_MODEL, D_FF), D_MODEL),
        'w2_up': normal(ks[14], (L, D_MODEL, D_FF), D_MODEL),
        'w2_down': normal(ks[15], (L, D_FF, D_MODEL), D_FF),
        'g_final': gain(ks[16], (D_MODEL,)),
    }


def reference(x, g_ffn1, w1_gate, w1_up, w1_down, g_mix, w_in, q_norm_a, k_norm_a,
              w_branch_a, w_branch_b, w_out, g_ffn2, w2_gate, w2_up, w2_down, g_final):
    h = x
    for l in range(DEPTH):
        h = hybrid_layer(h, g_ffn1[l], w1_gate[l], w1_up[l], w1_down[l], g_mix[l], w_in[l],
                         q_norm_a[l], k_norm_a[l], w_branch_a[l], w_branch_b[l], w_out[l],
                         g_ffn2[l], w2_gate[l], w2_up[l], w2_down[l])
    return rms_norm(h, g_final)
```

```python
import math
import numpy as np
import ml_dtypes
import concourse.bass as bass
import concourse.mybir as mybir
from concourse.bass_utils import run_bass_kernel_spmd

F32 = mybir.dt.float32
BF16 = mybir.dt.bfloat16
AF = mybir.ActivationFunctionType
ALU = mybir.AluOpType
AX = mybir.AxisListType

D = 4096
T = 4096
NB = 4
DFF = 11008
HD = 128
AH = 16
AKV = 4
BG = 3
BHG = 8
BH = 24
DIL = (1, 4, 16)
C0 = 2048
C1 = C0 + 512
C2 = C1 + 512
C3 = C2 + 9216
INW = C3 + 2 * D
TT = 512
OWN = 2048
EPS = 1e-6
SCALE = HD ** -0.5
KC_D = D // 128
KC_F = DFF // 128
NSLOT = 3
PREF = 2


class Buf:
    __slots__ = ("name", "w", "r", "chan")

    def __init__(self, name):
        self.name = name
        self.w = {}
        self.r = {}
        self.chan = None


class Prog:
    def __init__(self):
        self.ops = {e: [] for e in ("pe", "act", "dve", "pool", "sp")}
        self.nchan = 0
        self.chan_val = []
        self.waited = {e: {} for e in self.ops}

    def alias(self, new, olds):
        for o in olds:
            for k, v in o.w.items():
                if new.r.get(k, -1) < v:
                    new.r[k] = v
            for k, v in o.r.items():
                if new.r.get(k, -1) < v:
                    new.r[k] = v
        return new

    def op(self, eng, fn, reads=(), writes=(), pwrites=(), dma=0, chan_buf=None):
        waits = {}

        def need(evs):
            for k, v in evs.items():
                if waits.get(k, -1) < v:
                    waits[k] = v

        for b in reads:
            need(b.w)
        for b in writes:
            need(b.w)
            need(b.r)
        for b in pwrites:
            need(b.r)
        if eng == "pe":
            waits.pop("pe", None)
        wl = []
        wd = self.waited[eng]
        for k, v in waits.items():
            if wd.get(k, -1) >= v:
                continue
            wd[k] = v
            wl.append((k, v))
        lst = self.ops[eng]
        seq = len(lst)
        if dma:
            if chan_buf.chan is None:
                chan_buf.chan = self.nchan
                self.nchan += 1
                self.chan_val.append(0)
            c = chan_buf.chan
            self.chan_val[c] += 16 * dma
            ev = (("d", c), self.chan_val[c])
        else:
            ev = (eng, seq)
        lst.append({"fn": fn, "waits": wl, "dma": dma, "chan": chan_buf.chan if dma else None, "sig": False})
        for b in reads:
            if b.r.get(ev[0], -1) < ev[1]:
                b.r[ev[0]] = ev[1]
        for b in writes:
            b.w = {ev[0]: ev[1]}
            b.r = {}
        for b in pwrites:
            if b.w.get(ev[0], -1) < ev[1]:
                b.w[ev[0]] = ev[1]

    def finalize(self):
        for e, lst in self.ops.items():
            for o in lst:
                for k, v in o["waits"]:
                    if not isinstance(k, tuple):
                        self.ops[k][v]["sig"] = True
        self.cnt = {}
        for e, lst in self.ops.items():
            c = 0
            arr = []
            for o in lst:
                if o["sig"] and not o["dma"]:
                    c += 1
                arr.append(c)
            self.cnt[e] = arr

    def replay(self, eng, e, esem, csems):
        for o in self.ops[eng]:
            for k, v in o["waits"]:
                if isinstance(k, tuple):
                    e.wait_ge(csems[k[1]], v)
                else:
                    e.wait_ge(esem[k], self.cnt[k][v])
            r = o["fn"](e)
            if o["dma"]:
                if not isinstance(r, (list, tuple)):
                    r = [r]
                assert len(r) == o["dma"], (len(r), o["dma"])
                for ins in r:
                    ins.then_inc(csems[o["chan"]], 16)
            elif o["sig"]:
                if isinstance(r, (list, tuple)):
                    r = r[-1]
                r.then_inc(esem[eng], 1)


def build(debug=None):
    dbg = debug or {}
    nc = bass.Bass("TRN2", target_bir_lowering=False)
    P = Prog()

    def dram_in(name, shape, dt=F32):
        return nc.dram_tensor(name, list(shape), dt, kind="ExternalInput").ap()

    def dram_tmp(name, shape, dt):
        kind = "ExternalOutput" if name in dbg.get("dump", ()) else "Internal"
        return nc.dram_tensor(name, list(shape), dt, kind=kind).ap()

    x_d = dram_in("x", [T, D])
    wsrc = {
        "w1g": dram_in("w1g", [D, DFF]), "w1u": dram_in("w1u", [D, DFF]), "w1d": dram_in("w1d", [DFF, D]),
        "win": dram_in("win", [D, INW]),
        "wba": dram_in("wba", [2048, D]), "wbb": dram_in("wbb", [1024, D]), "wo": dram_in("wo", [D, D]),
        "w2g": dram_in("w2g", [D, DFF]), "w2u": dram_in("w2u", [D, DFF]), "w2d": dram_in("w2d", [DFF, D]),
    }
    gvec = {k: dram_in(k, [1, D]) for k in ("g1", "gm", "g2", "gf")}
    qn_d = dram_in("qn", [1, HD])
    kn_d = dram_in("kn", [1, HD])
    cos_d = dram_in("cosr", [T, 256])
    sin_d = dram_in("sinr", [T, 256])
    etab_d = dram_in("etab", [BH, 128, 256])
    idn_d = dram_in("idn", [128, 128], BF16)
    out_d = nc.dram_tensor("out", [OWN, D], F32, kind="ExternalOutput").ap()

    wb = {k: dram_tmp(k + "_b", v.shape, BF16) for k, v in wsrc.items()}
    H1 = dram_tmp("H1", [T, D], F32)
    H2 = dram_tmp("H2", [OWN, D], F32)
    H3 = dram_tmp("H3", [OWN, D], F32)
    QA = dram_tmp("QA", [AH, 128, OWN], BF16)
    KA = dram_tmp("KA", [AKV, 128, T], BF16)
    VA = dram_tmp("VA", [T, 512], BF16)
    QB = dram_tmp("QB", [BH, 128, OWN], BF16)
    KB = dram_tmp("KB", [BH, 128, T], BF16)
    VB = dram_tmp("VB", [T, BH * 128], BF16)
    SG = dram_tmp("SG", [2, 32, 128, OWN], BF16)
    YA = dram_tmp("YA", [AH, 128, OWN], BF16)
    YB = dram_tmp("YB", [BHG, 128, OWN], BF16)


    from contextlib import ExitStack
    es = ExitStack()
    E = es.enter_context
    actT = E(nc.sbuf_tensor("actT", [128, KC_D, TT], BF16))
    big = E(nc.sbuf_tensor("big", [128, KC_F * TT], BF16))
    ring = [E(nc.sbuf_tensor(f"ring{i}", [128, 4096], BF16)) for i in range(NSLOT)]
    stg = [E(nc.sbuf_tensor(f"stg{i}", [128, 2048], F32)) for i in range(2)]
    silu_t = E(nc.sbuf_tensor("silu_t", [128, 1024], F32))
    cs_t = E(nc.sbuf_tensor("cs_t", [128, 2, 4, 256], F32))
    ident = E(nc.sbuf_tensor("ident", [128, 128], BF16))
    ones = E(nc.sbuf_tensor("ones", [128, 128], BF16))
    qkn = E(nc.sbuf_tensor("qkn", [128, 2, 128], F32))
    small = E(nc.sbuf_tensor("small", [128, 64], F32))
    ps = E(nc.psum_tensor("ps", [128, 4096], F32))

    def bank(i):
        return ps[:, i * 512:(i + 1) * 512]

    bankb = [Buf(f"bank{i}") for i in range(8)]
    hT = big[:, :].rearrange("p (c t) -> p c t", t=TT)
    big32 = big[:, :].bitcast(F32)
    xrow = [big32[:, 0:4096], big32[:, 4096:8192]]
    grep = big32[:, 8192:12288]
    junk = big[:, 24576:28672]
    xnp = [big[:, 28672:29696], big[:, 29696:30720]]

    B_hT = Buf("hT")
    B_actT = Buf("actT")
    B_ring = [Buf(f"ring{i}") for i in range(NSLOT)]
    B_stg = [Buf(f"stg{i}") for i in range(2)]
    B_silu = Buf("silu")
    B_cs = Buf("cs")
    B_const = Buf("const")
    B_small = Buf("small")
    B_sm = [Buf(f"sm{i}") for i in range(4)]
    B_xrow = [Buf("xrow0"), Buf("xrow1")]
    B_grep = Buf("grep")
    B_junk = Buf("junk")
    B_xnp = [Buf("xnp0"), Buf("xnp1")]
    B_w = {k: Buf("w_" + k) for k in wsrc}
    B_x = Buf("x_in")
    B_H1 = [Buf(f"H1_{t}") for t in range(8)]
    B_H2 = [Buf(f"H2_{t}") for t in range(4)]
    B_H3 = [Buf(f"H3_{t}") for t in range(4)]
    B_QA, B_KA, B_VA, B_QB, B_KB, B_VB, B_SG, B_YA, B_YB = (Buf(n) for n in
                                                           ("QA", "KA", "VA", "QB", "KB", "VB", "SG", "YA", "YB"))
    B_out = Buf("out")
    rr = {"bank": 0, "stg": 0, "f32": 0, "ce": 0}
    f32s = [E(nc.sbuf_tensor(f"f32s{i}", [128, 2048], F32)) for i in range(2)]
    B_f32s = [Buf("f32s0"), Buf("f32s1")]

    def emit_cast(k):
        src, dst = wsrc[k], wb[k]
        R, Cc = src.shape
        rows = max(1, min(R, (6 << 20) // (Cc * 4)))
        r0 = 0
        while r0 < R:
            r1 = min(R, r0 + rows)
            P.op("pool", lambda e, a=r0, b=r1, s=src, d=dst: e.dma_start(out=d[a:b, :], in_=s[a:b, :]),
                 pwrites=[B_w[k]], dma=1, chan_buf=B_w[k])
            r0 = r1

    bg_units = []
    for k in ("wba", "wbb", "wo", "w2g", "w2u", "w2d"):
        R_, C_ = wsrc[k].shape
        for r0 in range(0, R_, 128):
            for c0 in range(0, C_, 2048):
                bg_units.append((k, r0, c0, min(2048, C_ - c0)))
    bg_state = {"on": False, "cnt": 0, "pend": None}

    def bg_tick():
        if not bg_state["on"]:
            return
        bg_state["cnt"] += 1
        if bg_state["cnt"] % 2:
            return
        if bg_state["pend"] is not None:
            bg_state["pend"]()
            bg_state["pend"] = None
        if not bg_units:
            return
        k, r0, c0, w = bg_units.pop(0)
        fs, B_fs = f32s[0], B_f32s[0]
        os_, B_os = f32s[1][:, :].bitcast(BF16), B_f32s[1]
        P.op("sp", lambda e: e.dma_start(out=fs[:, 0:w], in_=wsrc[k][r0:r0 + 128, c0:c0 + w]),
             writes=[B_fs], dma=1, chan_buf=B_fs)
        if (bg_state["cnt"] // 2) % 2:
            P.op("dve", lambda e: e.tensor_copy(out=os_[:, 0:w], in_=fs[:, 0:w]), reads=[B_fs], writes=[B_os])
        else:
            P.op("act", lambda e: e.activation(out=os_[:, 0:w], in_=fs[:, 0:w], func=AF.Copy), reads=[B_fs], writes=[B_os])

        def store():
            P.op("sp", lambda e: e.dma_start(out=wb[k][r0:r0 + 128, c0:c0 + w], in_=os_[:, 0:w]),
                 reads=[B_os], pwrites=[B_w[k]], dma=1, chan_buf=B_os)
        bg_state["pend"] = store

    P.op("sp", lambda e: [e.dma_start(out=ident[:], in_=idn_d[:]),
                          e.dma_start(out=qkn[:, 0, :], in_=qn_d[0:1, :].partition_broadcast(128)),
                          e.dma_start(out=qkn[:, 1, :], in_=kn_d[0:1, :].partition_broadcast(128))],
         writes=[B_const], dma=3, chan_buf=B_const)
    B_ones = Buf("ones")
    P.op("dve", lambda e: e.memset(ones[:], 1.0), writes=[B_ones])

    steps = []

    def run_steps():
        n = len(steps)
        chunk_ids = [i for i in range(n) if steps[i][0] is not None]
        slot_of = {ci: j % NSLOT for j, ci in enumerate(chunk_ids)}
        issued = 0
        pending_st = []

        def issue(j):
            ci = chunk_ids[j]
            dfn, wkeys, ndma, _ = steps[ci]
            sl = slot_of[ci]
            while pending_st:
                pending_st.pop(0)()
            if isinstance(dfn, tuple):
                dfn[1](ring[sl], B_ring[sl], pending_st)
                return
            P.op("sp", lambda e, f=dfn, s=sl: f(e, ring[s]), reads=[B_w[k] for k in wkeys],
                 writes=[B_ring[sl]], dma=ndma, chan_buf=B_ring[sl])

        pos = {ci: j for j, ci in enumerate(chunk_ids)}
        for i in range(n):
            if steps[i][0] is not None:
                j = pos[i]
                while issued <= min(j + PREF, len(chunk_ids) - 1):
                    issue(issued)
                    issued += 1
                steps[i][3](ring[slot_of[i]], B_ring[slot_of[i]])
                bg_tick()
            else:
                steps[i][3](None, None)
        while pending_st:
            pending_st.pop(0)()
        assert not bg_units, len(bg_units)
        if bg_state["pend"] is not None:
            bg_state["pend"]()
            bg_state["pend"] = None
        steps.clear()

    def add_step(body, dfn=None, wkeys=(), ndma=1):
        steps.append((dfn, wkeys, ndma, body))

    def gemm(kind, src, B_src, KC, slab_list, epilogue, fu=False):
        pend = [None]
        deferred = []
        for slab in slab_list:
            pset = rr["bank"]
            rr["bank"] ^= 1
            banks = [pset * 4 + a for a in range(4)]
            ngrp = (KC + 7) // 8
            for kg in range(ngrp):
                k0 = kg * 8
                kn = min(8, KC - k0)
                if kind == "FM":
                    def dfn(e, slot, slab=slab, k0=k0, kn=kn):
                        sv = slot[:, :].rearrange("p (k c) -> p k c", c=512)
                        res = []
                        a = 0
                        while a < 4:
                            b = a
                            while b + 1 < 4 and slab[b + 1][0] == slab[a][0] and slab[b + 1][1] == slab[b][1] + 128:
                                b += 1
                            wkey, c0 = slab[a]
                            wsrc_ap = wb[wkey][k0 * 128:(k0 + kn) * 128, c0:c0 + 128 * (b - a + 1)]
                            res.append(e.dma_start(out=sv[:, 0:kn, a * 128:(b + 1) * 128],
                                                   in_=wsrc_ap.rearrange("(k p) c -> p k c", p=128)))
                            a = b + 1
                        return res
                    nd = 0
                    a = 0
                    while a < 4:
                        b = a
                        while b + 1 < 4 and slab[b + 1][0] == slab[a][0] and slab[b + 1][1] == slab[b][1] + 128:
                            b += 1
                        nd += 1
                        a = b + 1
                    wkeys = sorted(set(s[0] for s in slab))
                else:
                    def dfn(e, slot, slab=slab, k0=k0, kn=kn):
                        sv = slot[:, :].rearrange("p (k c) -> p k c", c=512)
                        wkey, c0 = slab
                        return [e.dma_start(out=sv[:, 0:kn, :],
                                            in_=wb[wkey][k0 * 128:(k0 + kn) * 128, c0:c0 + 512].rearrange(
                                                "(k p) c -> p k c", p=128))]
                    nd = 1
                    wkeys = [slab[0]]

                def body(slot, B_slot, k0=k0, kn=kn, banks=banks, KC=KC):
                    sv = slot[:, :].rearrange("p (k c) -> p k c", c=512)
                    for kk in range(kn):
                        kc = k0 + kk
                        for a in range(4):
                            if kind == "FM":
                                lhsT, rhs = sv[:, kk, a * 128:(a + 1) * 128], src[:, kc, :]
                            else:
                                lhsT, rhs = src[:, kc, a * 128:(a + 1) * 128], sv[:, kk, :]
                            P.op("pe", lambda e, o=bank(banks[a]), l=lhsT, r=rhs, st=(kc == 0), sp=(kc == KC - 1):
                                 e.matmul(o, l, r, start=st, stop=sp),
                                 reads=[B_slot, B_src], writes=[bankb[banks[a]]])
                if fu:
                    if kind == "FM":
                        parts = []
                        a = 0
                        while a < 4:
                            b = a
                            while b + 1 < 4 and slab[b + 1][0] == slab[a][0] and slab[b + 1][1] == slab[b][1] + 128:
                                b += 1
                            parts.append((slab[a][0], slab[a][1], a * 128, (b - a + 1) * 128))
                            a = b + 1
                    else:
                        parts = [(slab[0], slab[1], 0, 512)]

                    def fu_issue(slot, B_slot, pending_st, parts=parts, k0=k0, kn=kn):
                        sv = slot[:, :].rearrange("p (k c) -> p k c", c=512)
                        for hf in range((kn + 3) // 4):
                            ka = hf * 4
                            kb = min(kn, ka + 4)
                            fi = rr["f32"]
                            rr["f32"] ^= 1
                            fs, B_fs = f32s[fi], B_f32s[fi]
                            fv = fs[:, :].rearrange("p (k c) -> p k c", c=512)
                            P.op("sp", lambda e, ka=ka, kb=kb, fv=fv: [
                                e.dma_start(out=fv[:, 0:kb - ka, so:so + w],
                                            in_=wsrc[wk][(k0 + ka) * 128:(k0 + kb) * 128, c0:c0 + w].rearrange(
                                                "(k p) c -> p k c", p=128)) for (wk, c0, so, w) in parts],
                                 writes=[B_fs], dma=len(parts), chan_buf=B_fs)
                            ce = "dve" if rr["ce"] == 0 else "act"
                            rr["ce"] ^= 1
                            wr = dict(writes=[B_slot]) if hf == 0 else dict(pwrites=[B_slot])
                            if ce == "dve":
                                P.op("dve", lambda e, ka=ka, kb=kb, fv=fv: e.tensor_copy(out=sv[:, ka:kb, :], in_=fv[:, 0:kb - ka, :]),
                                     reads=[B_fs], **wr)
                            else:
                                P.op("act", lambda e, ka=ka, kb=kb, fv=fv: e.activation(out=sv[:, ka:kb, :], in_=fv[:, 0:kb - ka, :], func=AF.Copy),
                                     reads=[B_fs], **wr)

                        def store():
                            P.op("sp", lambda e: [
                                e.dma_start(out=wb[wk][k0 * 128:(k0 + kn) * 128, c0:c0 + w].rearrange("(k p) c -> p k c", p=128),
                                            in_=sv[:, 0:kn, so:so + w]) for (wk, c0, so, w) in parts],
                                 reads=[B_slot], pwrites=[B_w[wk] for (wk, _, _, _) in parts], dma=len(parts), chan_buf=B_slot)
                        pending_st.append(store)
                    add_step(body, (None, fu_issue), wkeys, nd)
                else:
                    add_step(body, dfn, wkeys, nd)
            if pend[0] is not None:
                add_step(pend[0])
                pend[0] = None

            def ep_step(slot, B_slot, slab=slab, banks=banks):
                d = epilogue(slab, banks)
                if d is not None:
                    deferred.append(d)
            add_step(ep_step)

            def pend_step(slot, B_slot):
                while deferred:
                    deferred.pop(0)()
            pend[0] = pend_step
        if pend[0] is not None:
            add_step(pend[0])

    def norm_tile(src_d, row0, B_src, gkey, final_out=None):
        add_step(lambda slot, B_slot: _norm_tile(src_d, row0, B_src, gkey, final_out))

    def _norm_tile(src_d, row0, B_src, gkey, final_out=None):
        for b in B_xrow + [B_grep, B_junk] + B_xnp:
            P.alias(b, [B_hT])
        P.op("sp", lambda e: e.dma_start(out=grep, in_=gvec[gkey][0:1, :].partition_broadcast(128)),
             writes=[B_grep], dma=1, chan_buf=B_grep)
        for s in range(4):
            xr, B_xr = xrow[s % 2], B_xrow[s % 2]
            r0 = row0 + s * 128
            P.op("sp", lambda e, xr=xr, r0=r0: e.dma_start(out=xr, in_=src_d[r0:r0 + 128, :]),
                 reads=[B_src], writes=[B_xr], dma=1, chan_buf=B_xr)
            ssc = small[:, s:s + 1]
            sdc = small[:, 8 + s:9 + s]
            rsc = small[:, 16 + s:17 + s]
            P.op("act", lambda e, xr=xr, ssc=ssc: e.activation(out=junk, in_=xr, func=AF.Square, scale=float(D ** -0.5),
                                                              accum_out=ssc),
                 reads=[B_xr], writes=[B_junk, B_sm[s]])
            P.op("act", lambda e, ssc=ssc, sdc=sdc: e.activation(out=sdc, in_=ssc, func=AF.Sqrt, bias=EPS, scale=1.0),
                 reads=[B_sm[s]], writes=[B_sm[s]])
            P.op("dve", lambda e, sdc=sdc, rsc=rsc: e.reciprocal(out=rsc, in_=sdc), reads=[B_sm[s]], writes=[B_sm[s]])
            if final_out is not None:
                P.op("dve", lambda e, xr=xr, rsc=rsc: e.scalar_tensor_tensor(out=xr, in0=xr, scalar=rsc, in1=grep,
                                                                            op0=ALU.mult, op1=ALU.mult),
                     reads=[B_sm[s], B_grep], writes=[B_xr])
                P.op("sp", lambda e, xr=xr, r0=r0: e.dma_start(out=final_out[r0:r0 + 128, :], in_=xr),
                     reads=[B_xr], pwrites=[B_out], dma=1, chan_buf=B_xr)
                continue
            for pc in range(4):
                xp, B_xp = xnp[pc % 2], B_xnp[pc % 2]
                P.op("dve", lambda e, xr=xr, rsc=rsc, xp=xp, pc=pc: e.scalar_tensor_tensor(
                    out=xp, in0=xr[:, pc * 1024:(pc + 1) * 1024], scalar=rsc, in1=grep[:, pc * 1024:(pc + 1) * 1024],
                    op0=ALU.mult, op1=ALU.mult), reads=[B_xr, B_sm[s], B_grep], writes=[B_xp])
                bk = rr["bank"] * 4 + (pc % 4)
                pb = bank(bk).bitcast(BF16)
                for c in range(8):
                    P.op("pe", lambda e, o=pb[:, c * 128:(c + 1) * 128], i=xp[:, c * 128:(c + 1) * 128]:
                         e.transpose(o, i, ident[:]), reads=[B_xp, B_const], writes=[bankb[bk]])
                eng = "act" if pc % 2 == 0 else "dve"
                dst = actT[:, pc * 8:(pc + 1) * 8, s * 128:(s + 1) * 128]
                srcp = pb.rearrange("p (c t) -> p c t", t=128)
                if eng == "act":
                    P.op("act", lambda e, d=dst, s_=srcp: e.activation(out=d, in_=s_, func=AF.Copy),
                         reads=[bankb[bk]], pwrites=[B_actT])
                else:
                    P.op("dve", lambda e, d=dst, s_=srcp: e.tensor_copy(out=d, in_=s_),
                         reads=[bankb[bk]], pwrites=[B_actT])
            rr["bank"] ^= 1
        P.alias(B_hT, B_xrow + [B_grep, B_junk] + B_xnp)

    def ffn_tile(src_d, row0, B_src, gkey, wg, wu, wd, dst_d, drow0, B_dst, fu=False):
        norm_tile(src_d, row0, B_src, gkey)

        def ep_gu(slab, banks):
            f0 = slab[0][1] // 128
            nvalid = 2 if f0 + 2 <= KC_F else 1
            g_ps = ps[:, banks[0] * 512: banks[0] * 512 + 1024]
            u_ps = ps[:, banks[2] * 512: banks[2] * 512 + 1024]
            P.op("act", lambda e: e.activation(out=silu_t[:], in_=g_ps, func=AF.Silu),
                 reads=[bankb[banks[0]], bankb[banks[1]]], writes=[B_silu])
            P.op("dve", lambda e: e.tensor_tensor(out=hT[:, f0:f0 + 2, :].rearrange("p c t -> p (c t)"), in0=silu_t[:],
                                                  in1=u_ps, op=ALU.mult),
                 reads=[B_silu, bankb[banks[2]], bankb[banks[3]]], pwrites=[B_hT])

        slabs = []
        for f0 in range(0, KC_F, 2):
            c = f0 * 128
            slabs.append([(wg, c), (wg, c + 128), (wu, c), (wu, c + 128)])
        gemm("FM", actT, B_actT, KC_D, slabs, ep_gu, fu=fu)

        def ep_down(slab, banks):
            c0 = slab[1]
            si = rr["stg"]
            rr["stg"] ^= 1
            st, B_st = stg[si], B_stg[si]
            stv = st[:, :].rearrange("p (s c) -> p s c", c=512)
            P.op("sp", lambda e: e.dma_start(out=stv, in_=src_d[row0:row0 + 512, c0:c0 + 512].rearrange(
                "(s p) c -> p s c", p=128)), reads=[B_src], writes=[B_st], dma=1, chan_buf=B_st)
            pss = ps[:, banks[0] * 512: banks[0] * 512 + 2048]
            P.op("dve", lambda e: e.scalar_tensor_tensor(out=st[:, :], in0=pss, scalar=0.5, in1=st[:, :],
                                                         op0=ALU.mult, op1=ALU.add),
                 reads=[bankb[b] for b in banks], writes=[B_st])
            P.op("sp", lambda e: e.dma_start(out=dst_d[drow0:drow0 + 512, c0:c0 + 512].rearrange(
                "(s p) c -> p s c", p=128), in_=stv), reads=[B_st], pwrites=[B_dst], dma=1, chan_buf=B_st)

        gemm("TM", hT, B_hT, KC_F, [(wd, c) for c in range(0, D, 512)], ep_down, fu=fu)

    qrot = E(nc.sbuf_tensor("qrot", [128, 2048], BF16))
    tstage = [E(nc.sbuf_tensor(f"tstage{i}", [128, 2048], BF16)) for i in range(2)]
    B_qrot = Buf("qrot")
    B_tst = [Buf("tst0"), Buf("tst1")]
    rr["tst"] = 0

    def next_tst():
        i = rr["tst"]
        rr["tst"] ^= 1
        return tstage[i], B_tst[i]

    def ps4(banks):
        return ps[:, banks[0] * 512: banks[0] * 512 + 2048]

    def inproj_tile(t):
        own = t < 4
        needB = t < 6
        row0 = t * TT
        norm_tile(H1, row0, B_H1[t], "gm")

        def ld_cs(slot, B_slot):
            P.op("sp", lambda e: [e.dma_start(out=cs_t[:, 0], in_=cos_d[row0:row0 + 512, :].rearrange("(s p) c -> p s c", p=128)),
                                  e.dma_start(out=cs_t[:, 1], in_=sin_d[row0:row0 + 512, :].rearrange("(s p) c -> p s c", p=128))],
                 writes=[B_cs], dma=2, chan_buf=B_cs)
        add_step(ld_cs)

        def ep_qk(which, head0, dst, B_dst):
            def ep(slab, banks):
                pv = ps4(banks)
                pv3 = pv.rearrange("p (a d) -> p a d", d=128)
                sA, B_sA = stg[0], B_stg[0]
                sB, B_sB = stg[1], B_stg[1]
                sA3 = sA[:, :].rearrange("p (a d) -> p a d", d=128)
                ssq = small[:, 24:40]
                P.op("act", lambda e: e.activation(out=sA[:, :], in_=pv, func=AF.Square, scale=float(HD ** -0.5)),
                     reads=[bankb[b] for b in banks], writes=[B_sA])
                P.op("dve", lambda e: e.tensor_reduce(out=ssq, in_=sA3, axis=AX.X, op=ALU.add),
                     reads=[B_sA], writes=[B_small])
                P.op("act", lambda e: e.activation(out=ssq, in_=ssq, func=AF.Sqrt, bias=EPS, scale=1.0),
                     reads=[B_small], writes=[B_small])
                P.op("dve", lambda e: e.reciprocal(out=ssq, in_=ssq), reads=[B_small], writes=[B_small])
                P.op("dve", lambda e: e.tensor_tensor(out=sA3, in0=pv3, in1=ssq.unsqueeze(2).to_broadcast([128, 16, 128]),
                                                      op=ALU.mult),
                     reads=[bankb[b] for b in banks] + [B_small], writes=[B_sA])
                P.op("dve", lambda e: e.tensor_tensor(out=sA3, in0=sA3,
                                                      in1=qkn[:, which:which + 1, :].to_broadcast([128, 16, 128]),
                                                      op=ALU.mult), reads=[B_const], writes=[B_sA])
                x5 = sA[:, :].rearrange("p (s h two i) -> p s h two i", s=4, h=8, two=2, i=32)
                x1, x2 = x5[:, :, :, 0, :], x5[:, :, :, 1, :]
                o5 = qrot[:, :].rearrange("p (s h two i) -> p s h two i", s=4, h=8, two=2, i=32)
                o1, o2 = o5[:, :, :, 0, :], o5[:, :, :, 1, :]
                Cv = cs_t[:, 0].rearrange("p s (h i) -> p s h i", i=32)
                Sv = cs_t[:, 1].rearrange("p s (h i) -> p s h i", i=32)
                tA = sB[:, 0:1024].rearrange("p (s h i) -> p s h i", s=4, h=8)
                tB = sB[:, 1024:2048].rearrange("p (s h i) -> p s h i", s=4, h=8)
                P.op("dve", lambda e: e.tensor_tensor(out=tA, in0=x1, in1=Cv, op=ALU.mult), reads=[B_sA, B_cs], writes=[B_sB])
                P.op("dve", lambda e: e.tensor_tensor(out=tB, in0=x2, in1=Sv, op=ALU.mult), reads=[B_sA, B_cs], writes=[B_sB])
                P.op("dve", lambda e: e.tensor_tensor(out=o1, in0=tA, in1=tB, op=ALU.subtract), reads=[B_sB], writes=[B_qrot])
                P.op("dve", lambda e: e.tensor_tensor(out=tA, in0=x1, in1=Sv, op=ALU.mult), reads=[B_sA, B_cs], writes=[B_sB])
                P.op("dve", lambda e: e.tensor_tensor(out=tB, in0=x2, in1=Cv, op=ALU.mult), reads=[B_sA, B_cs], writes=[B_sB])
                P.op("dve", lambda e: e.tensor_tensor(out=o2, in0=tA, in1=tB, op=ALU.add), reads=[B_sB, B_qrot], writes=[B_qrot])

                def deferred():
                    pT = ps[:, banks[0] * 512: banks[0] * 512 + 1024].bitcast(BF16)
                    for hh in range(4):
                        for a in range(4):
                            P.op("pe", lambda e, o=pT[:, hh * 512 + a * 128: hh * 512 + (a + 1) * 128],
                                 i=qrot[:, (a * 4 + hh) * 128:(a * 4 + hh + 1) * 128]: e.transpose(o, i, ident[:]),
                                 reads=[B_qrot, B_const], writes=[bankb[banks[0]], bankb[banks[1]]])
                    ts, B_ts = next_tst()
                    P.op("act", lambda e: e.activation(out=ts[:, :], in_=pT, func=AF.Copy),
                         reads=[bankb[banks[0]], bankb[banks[1]]], writes=[B_ts])
                    P.op("sp", lambda e: e.dma_start(
                        out=dst[head0:head0 + 4, :, row0:row0 + 512].rearrange("h p c -> p h c"),
                        in_=ts[:, :].rearrange("p (h c) -> p h c", c=512)), reads=[B_ts], pwrites=[B_dst], dma=1, chan_buf=B_ts)
                return deferred
            return ep

        def ep_v(dst, dcol, B_dst):
            def ep(slab, banks):
                ts, B_ts = next_tst()
                P.op("act", lambda e: e.activation(out=ts[:, :], in_=ps4(banks), func=AF.Copy),
                     reads=[bankb[b] for b in banks], writes=[B_ts])
                P.op("sp", lambda e: e.dma_start(
                    out=dst[row0:row0 + 512, dcol:dcol + 512].rearrange("(s p) c -> p s c", p=128),
                    in_=ts[:, :].rearrange("p (s c) -> p s c", c=512)), reads=[B_ts], pwrites=[B_dst], dma=1, chan_buf=B_ts)
            return ep

        def ep_fm(dst3, B_dst, func):
            def ep(slab, banks):
                ts, B_ts = next_tst()
                P.op("act", lambda e: e.activation(out=ts[:, :], in_=ps4(banks), func=func),
                     reads=[bankb[b] for b in banks], writes=[B_ts])
                P.op("sp", lambda e: e.dma_start(out=dst3.rearrange("h p c -> p h c"),
                                                 in_=ts[:, :].rearrange("p (h c) -> p h c", c=512)),
                     reads=[B_ts], pwrites=[B_dst], dma=1, chan_buf=B_ts)
            return ep

        if own:
            for gi in range(4):
                gemm("TM", actT, B_actT, KC_D, [("win", gi * 512)], ep_qk(0, gi * 4, QA, B_QA), fu=(t == 0))
        gemm("TM", actT, B_actT, KC_D, [("win", C0)], ep_qk(1, 0, KA, B_KA), fu=(t == 0))
        gemm("TM", actT, B_actT, KC_D, [("win", C1)], ep_v(VA, 0, B_VA), fu=(t == 0))
        if needB:
            for g in range(BG):
                for hh in range(2):
                    gemm("TM", actT, B_actT, KC_D, [("win", C2 + (g * 3 + 2) * 1024 + hh * 512)],
                         ep_v(VB, (g * 8 + hh * 4) * 128, B_VB), fu=(t == 0))
        def fm_slab(c0):
            return [("win", c0 + a * 128) for a in range(4)]
        if needB:
            for g in range(BG):
                for hh in range(2):
                    h0 = g * 8 + hh * 4
                    gemm("FM", actT, B_actT, KC_D, [fm_slab(C2 + (g * 3 + 1) * 1024 + hh * 512)],
                         ep_fm(KB[h0:h0 + 4, :, row0:row0 + 512], B_KB, AF.Copy), fu=(t == 0))
        if own:
            for g in range(BG):
                for hh in range(2):
                    h0 = g * 8 + hh * 4
                    gemm("FM", actT, B_actT, KC_D, [fm_slab(C2 + (g * 3 + 0) * 1024 + hh * 512)],
                         ep_fm(QB[h0:h0 + 4, :, row0:row0 + 512], B_QB, AF.Copy), fu=(t == 0))
            for which in range(2):
                for j in range(8):
                    gemm("FM", actT, B_actT, KC_D, [fm_slab(C3 + which * D + j * 512)],
                         ep_fm(SG[which, j * 4:(j + 1) * 4, :, row0:row0 + 512], B_SG, AF.Sigmoid), fu=(t == 0))

    def attention():
        kT = big[:, 0:4096]
        vsb = big[:, 4096:8192].rearrange("p (b d) -> p b d", d=128)
        qT = [big[:, 8192:8704], big[:, 8704:9216]]
        pT = [big[:, 9216 + i * 512: 9216 + (i + 1) * 512] for i in range(3)]
        yT = [big[:, 10752:11264], big[:, 11264:11776]]
        rden = big32[:, 5888:6400]
        B_kT, B_vsb = Buf("kT"), Buf("vsb")
        B_qT = [Buf("qT0"), Buf("qT1")]
        B_pT = [Buf(f"pT{i}") for i in range(3)]
        B_yT = [Buf("yT0"), Buf("yT1")]
        B_rden = Buf("rden")
        allb = [B_kT, B_vsb, B_rden] + B_qT + B_pT + B_yT
        acc = big32[:, 8192:12288].rearrange("p (two t) -> p two t", two=2)
        qTb = big[:, 24576:26624]
        kTb = big[:, 26624:30720]
        etb = big32[:, 15360:15616]
        vb = big[:, 31232:33536].rearrange("p (b d) -> p b d", d=128)
        ptmp = big32[:, 16768:17024]
        pm = [big[:, 34048:34304], big[:, 34304:34560]]
        rdb = big32[:, 17280:19328]
        ybT = big[:, 38656:40704]
        B_acc, B_qTb, B_kTb, B_etb, B_vb, B_ptmp, B_rdb, B_ybT = (Buf(n) for n in
                                                                 ("acc", "qTb", "kTb", "etb", "vb", "ptmp", "rdb", "ybT"))
        B_pm = [Buf("pm0"), Buf("pm1")]
        allb += [B_acc, B_qTb, B_kTb, B_etb, B_vb, B_ptmp, B_rdb, B_ybT] + B_pm
        for b in allb:
            P.alias(b, [B_hT])

        cnt = 0
        for kvh in range(AKV):
            P.op("sp", lambda e, kvh=kvh: e.dma_start(out=kT, in_=KA[kvh]), reads=[B_KA], writes=[B_kT], dma=1, chan_buf=B_kT)
            P.op("sp", lambda e, kvh=kvh: e.dma_start(out=vsb, in_=VA[:, kvh * 128:(kvh + 1) * 128].rearrange(
                "(b p) d -> p b d", p=128)), reads=[B_VA], writes=[B_vsb], dma=1, chan_buf=B_vsb)
            for g in range(4):
                head = kvh * 4 + g
                for qt in range(4):
                    qi = cnt % 2
                    cnt += 1
                    P.op("sp", lambda e, qi=qi, head=head, qt=qt: e.dma_start(out=qT[qi], in_=QA[head, :, qt * 512:(qt + 1) * 512]),
                         reads=[B_QA], writes=[B_qT[qi]], dma=1, chan_buf=B_qT[qi])
                    nb_, db_ = 4 + 2 * qi, 5 + 2 * qi

                    def s_mm(blk, qi=qi):
                        sb = blk % 3
                        P.op("pe", lambda e: e.matmul(bank(sb), kT[:, blk * 128:(blk + 1) * 128], qT[qi], start=True, stop=True),
                             reads=[B_kT, B_qT[qi]], writes=[bankb[sb]])
                    s_mm(0)
                    for blk in range(32):
                        if blk + 1 < 32:
                            s_mm(blk + 1)
                        sb = blk % 3
                        P.op("act", lambda e, sb=sb: e.activation(out=pT[sb], in_=bank(sb), func=AF.Exp, scale=float(SCALE)),
                             reads=[bankb[sb]], writes=[B_pT[sb]])
                        P.op("pe", lambda e, sb=sb, blk=blk, nb_=nb_: e.matmul(bank(nb_), vsb[:, blk, :], pT[sb], start=(blk == 0), stop=(blk == 31)),
                             reads=[B_vsb, B_pT[sb]], writes=[bankb[nb_]])
                        P.op("pe", lambda e, sb=sb, blk=blk, db_=db_: e.matmul(bank(db_), ones[:], pT[sb], start=(blk == 0), stop=(blk == 31)),
                             reads=[B_ones, B_pT[sb]], writes=[bankb[db_]])
                    P.op("dve", lambda e, db_=db_: e.reciprocal(out=rden, in_=bank(db_)), reads=[bankb[db_]], writes=[B_rden])
                    P.op("dve", lambda e, nb_=nb_, qi=qi: e.tensor_tensor(out=yT[qi], in0=bank(nb_), in1=rden, op=ALU.mult),
                         reads=[bankb[nb_], B_rden], writes=[B_yT[qi]])
                    P.op("sp", lambda e, qi=qi, head=head, qt=qt: e.dma_start(out=YA[head, :, qt * 512:(qt + 1) * 512], in_=yT[qi]),
                         reads=[B_yT[qi]], pwrites=[B_YA], dma=1, chan_buf=B_yT[qi])

        ucnt = 0
        for i in range(BHG):
            for g in range(BG):
                h = g * 8 + i
                Dd = DIL[g]
                ML = OWN // Dd
                nblk = ML // 128 + 1
                pad = 64 * Dd
                P.op("sp", lambda e, h=h: e.dma_start(out=qTb, in_=QB[h]), reads=[B_QB], writes=[B_qTb], dma=1, chan_buf=B_qTb)
                P.op("dve", lambda e, pad=pad: e.memset(kTb[:, 0:pad], 0.0), writes=[B_kTb])
                P.op("sp", lambda e, h=h, pad=pad: e.dma_start(out=kTb[:, pad:pad + OWN + pad], in_=KB[h, :, 0:OWN + pad]),
                     reads=[B_KB], writes=[B_kTb], dma=1, chan_buf=B_kTb)
                P.op("sp", lambda e, h=h: e.dma_start(out=etb, in_=etab_d[h]), writes=[B_etb], dma=1, chan_buf=B_etb)
                for r in range(Dd):
                    base = r + pad
                    P.op("sp", lambda e, r=r, Dd=Dd, h=h, base=base, nblk=nblk: [
                        e.dma_start(out=vb[64:128, 0, :], in_=VB[r: r + 64 * Dd: Dd, h * 128:(h + 1) * 128]),
                        e.dma_start(out=vb[:, 1:nblk, :],
                                    in_=VB[base: base + Dd * 128 * (nblk - 1): Dd, h * 128:(h + 1) * 128].rearrange(
                                        "(j p) d -> p j d", p=128))],
                         reads=[B_VB], writes=[B_vb], dma=2, chan_buf=B_vb)
                    for qq in range(ML // 128):
                        m0 = 128 * qq
                        sbk = ucnt % 2
                        ndk = 2 + ucnt % 2
                        pmi = ucnt % 2
                        ucnt += 1
                        qsl = slice(r + Dd * m0, r + Dd * (m0 + 127) + 1, Dd)
                        for jj in range(2):
                            j = qq + jj
                            ksl = slice(r + 128 * Dd * j, r + 128 * Dd * j + Dd * 127 + 1, Dd)
                            P.op("pe", lambda e, sbk=sbk, jj=jj, ksl=ksl, qsl=qsl: e.matmul(
                                bank(sbk)[:, jj * 128:(jj + 1) * 128], kTb[:, ksl], qTb[:, qsl], start=True, stop=True),
                                reads=[B_kTb, B_qTb], writes=[bankb[sbk]])
                        P.op("act", lambda e, sbk=sbk: e.activation(out=ptmp, in_=bank(sbk)[:, 0:256], func=AF.Exp, scale=float(SCALE)),
                             reads=[bankb[sbk]], writes=[B_ptmp])
                        P.op("dve", lambda e, pmi=pmi: e.tensor_tensor(out=pm[pmi], in0=ptmp, in1=etb, op=ALU.mult),
                             reads=[B_ptmp, B_etb], writes=[B_pm[pmi]])
                        for jj in range(2):
                            j = qq + jj
                            lo = 64 if (j == 0) else 0
                            P.op("pe", lambda e, ndk=ndk, jj=jj, j=j, lo=lo, pmi=pmi: e.matmul(
                                bank(ndk)[:, 0:128], vb[lo:128, j, :], pm[pmi][lo:128, jj * 128:(jj + 1) * 128],
                                start=(jj == 0), stop=(jj == 1)), reads=[B_vb, B_pm[pmi]], writes=[bankb[ndk]])
                        for jj in range(2):
                            j = qq + jj
                            lo = 64 if (j == 0) else 0
                            P.op("pe", lambda e, ndk=ndk, jj=jj, lo=lo, pmi=pmi: e.matmul(
                                bank(ndk)[:, 128:256], ones[lo:128, :], pm[pmi][lo:128, jj * 128:(jj + 1) * 128],
                                start=(jj == 0), stop=(jj == 1)), reads=[B_ones, B_pm[pmi]], writes=[bankb[ndk]])
                        ndv = bank(ndk)[:, 0:256].rearrange("p (two q) -> p two q", two=2)
                        if g == 0:
                            P.op("dve", lambda e, ndv=ndv, qsl=qsl: e.tensor_copy(out=acc[:, :, qsl], in_=ndv),
                                 reads=[bankb[ndk]], pwrites=[B_acc])
                        else:
                            P.op("dve", lambda e, ndv=ndv, qsl=qsl: e.tensor_tensor(out=acc[:, :, qsl], in0=acc[:, :, qsl], in1=ndv, op=ALU.add),
                                 reads=[bankb[ndk], B_acc], writes=[B_acc])
            P.op("dve", lambda e: e.reciprocal(out=rdb, in_=acc[:, 1, :]), reads=[B_acc], writes=[B_rdb])
            P.op("dve", lambda e: e.tensor_tensor(out=ybT, in0=acc[:, 0, :], in1=rdb, op=ALU.mult), reads=[B_acc, B_rdb], writes=[B_ybT])
            P.op("sp", lambda e, i=i: e.dma_start(out=YB[i], in_=ybT), reads=[B_ybT], pwrites=[B_YB], dma=1, chan_buf=B_ybT)
        P.alias(B_hT, allb)

    def phase3_tile(t):
        row0 = t * TT
        yaT = big[:, 0:8192].rearrange("p (h c) -> p h c", c=512)
        ybT3 = big[:, 8192:12288].rearrange("p (h c) -> p h c", c=512)
        sgt = [big[:, 12288 + i * 2048: 12288 + (i + 1) * 2048] for i in range(2)]
        m1 = big32[:, 8192:10240]
        B_ya, B_yb3, B_m1 = Buf("yaT"), Buf("ybT3"), Buf("m1")
        B_sgt = [Buf("sgt0"), Buf("sgt1")]
        allb = [B_ya, B_yb3, B_m1] + B_sgt

        def start(slot, B_slot):
            for b in allb:
                P.alias(b, [B_hT])
            P.op("sp", lambda e: e.dma_start(out=yaT, in_=YA[:, :, row0:row0 + 512].rearrange("h p c -> p h c")),
                 reads=[B_YA], writes=[B_ya], dma=1, chan_buf=B_ya)
            P.op("sp", lambda e: e.dma_start(out=ybT3, in_=YB[:, :, row0:row0 + 512].rearrange("h p c -> p h c")),
                 reads=[B_YB], writes=[B_yb3], dma=1, chan_buf=B_yb3)
        add_step(start)

        for j in range(8):
            def epA(slab, banks, j=j):
                P.op("sp", lambda e: e.dma_start(out=sgt[0].rearrange("p (h c) -> p h c", c=512),
                                                 in_=SG[0, j * 4:(j + 1) * 4, :, row0:row0 + 512].rearrange("h p c -> p h c")),
                     reads=[B_SG], writes=[B_sgt[0]], dma=1, chan_buf=B_sgt[0])
                P.op("dve", lambda e: e.tensor_tensor(out=m1, in0=ps4(banks), in1=sgt[0], op=ALU.mult),
                     reads=[bankb[b] for b in banks] + [B_sgt[0]], writes=[B_m1])

            def epB(slab, banks, j=j):
                P.op("sp", lambda e: e.dma_start(out=sgt[1].rearrange("p (h c) -> p h c", c=512),
                                                 in_=SG[1, j * 4:(j + 1) * 4, :, row0:row0 + 512].rearrange("h p c -> p h c")),
                     reads=[B_SG], writes=[B_sgt[1]], dma=1, chan_buf=B_sgt[1])
                st, B_st = stg[0], B_stg[0]
                P.op("dve", lambda e: e.tensor_tensor(out=st[:, :], in0=ps4(banks), in1=sgt[1], op=ALU.mult),
                     reads=[bankb[b] for b in banks] + [B_sgt[1]], writes=[B_st])
                P.op("pool", lambda e: e.tensor_tensor(out=actT[:, j * 4:(j + 1) * 4, :].rearrange("p c t -> p (c t)"),
                                                       in0=m1, in1=st[:, :], op=ALU.add),
                     reads=[B_m1, B_st], pwrites=[B_actT])
            gemm("FM", yaT, B_ya, 16, [[("wba", j * 512 + a * 128) for a in range(4)]], epA)
            gemm("FM", ybT3, B_yb3, 8, [[("wbb", j * 512 + a * 128) for a in range(4)]], epB)

        def ep_out(slab, banks):
            c0 = slab[1]
            si = rr["stg"]
            rr["stg"] ^= 1
            st, B_st = stg[si], B_stg[si]
            stv = st[:, :].rearrange("p (s c) -> p s c", c=512)
            P.op("sp", lambda e: e.dma_start(out=stv, in_=H1[row0:row0 + 512, c0:c0 + 512].rearrange(
                "(s p) c -> p s c", p=128)), reads=[B_H1[t]], writes=[B_st], dma=1, chan_buf=B_st)
            P.op("dve", lambda e: e.tensor_tensor(out=st[:, :], in0=ps4(banks), in1=st[:, :], op=ALU.add),
                 reads=[bankb[b] for b in banks], writes=[B_st])
            P.op("sp", lambda e: e.dma_start(out=H2[row0:row0 + 512, c0:c0 + 512].rearrange(
                "(s p) c -> p s c", p=128), in_=stv), reads=[B_st], pwrites=[B_H2[t]], dma=1, chan_buf=B_st)
        gemm("TM", actT, B_actT, KC_D, [("wo", c) for c in range(0, D, 512)], ep_out)

        def end(slot, B_slot):
            P.alias(B_hT, allb)
        add_step(end)
        ffn_tile(H2, row0, B_H2[t], "g2", "w2g", "w2u", "w2d", H3, row0, B_H3[t])
        norm_tile(H3, row0, B_H3[t], "gf", final_out=out_d)

    ntile1 = dbg.get("ntile1", 8)
    for t in range(ntile1):
        ffn_tile(x_d, t * TT, B_x, "g1", "w1g", "w1u", "w1d", H1, t * TT, B_H1[t], fu=(t == 0))
        if dbg.get("stop") == "ffn1":
            continue
        inproj_tile(t)
        if t == 0:
            add_step(lambda slot, B_slot: bg_state.__setitem__("on", True))
    if dbg.get("stop") not in ("ffn1", "inproj"):
        def _attn_step(slot, B_slot):
            assert not bg_units and bg_state["pend"] is None or not bg_units, len(bg_units)
            if bg_state["pend"] is not None:
                bg_state["pend"]()
                bg_state["pend"] = None
            attention()
        add_step(_attn_step)
        if dbg.get("stop") != "attn":
            for t in range(4):
                phase3_tile(t)
    run_steps()
    fin_bufs = [B_out] + [b for b in (B_QA, B_KA, B_VA, B_QB, B_KB, B_VB, B_SG, B_YA, B_YB) if b.w] + \
        [b for b in B_H1 + B_H2 + B_H3 if b.w]
    P.op("sp", lambda e: e.wait_ge(fin_sem, 0), reads=fin_bufs)
    P.finalize()

    fin_sem = E(nc.semaphore("fin"))
    esem = {k: E(nc.semaphore("e_" + k)) for k in P.ops}
    csems = [E(nc.semaphore(f"c{i}")) for i in range(P.nchan)]
    block = E(nc.Block())

    @block.tensor
    def _(e):
        P.replay("pe", e, esem, csems)

    @block.scalar
    def _(e):
        P.replay("act", e, esem, csems)

    @block.vector
    def _(e):
        P.replay("dve", e, esem, csems)

    @block.gpsimd
    def _(e):
        P.replay("pool", e, esem, csems)

    @block.sync
    def _(e):
        P.replay("sp", e, esem, csems)

    es.close()
    return nc


def _consts():
    pos = np.arange(T)
    inv = (10000.0 ** (-np.arange(0, 64, 2, dtype=np.float32) / 64.0)).astype(np.float32)
    ang_r = (pos // 64).astype(np.float32)[:, None] * inv[None, :]
    ang_c = (pos % 64).astype(np.float32)[:, None] * inv[None, :]
    ang = np.concatenate([ang_r, ang_c], axis=1)
    cos = np.tile(np.cos(ang).astype(np.float32), (1, 4))
    sin = np.tile(np.sin(ang).astype(np.float32), (1, 4))
    slopes = np.exp2(-8.0 * np.arange(1, BH + 1, dtype=np.float32) / BH).astype(np.float32)
    kk = np.arange(128)[:, None, None]
    jj = np.arange(2)[None, :, None]
    qq = np.arange(128)[None, None, :]
    delta = kk - 64 - qq + 128 * jj
    valid = np.abs(delta) <= 64
    et = np.zeros((BH, 128, 2, 128), np.float32)
    for h in range(BH):
        Dd = DIL[h // 8]
        et[h] = np.where(valid, np.exp(-slopes[h] * np.float32(Dd) * np.abs(delta).astype(np.float32)), 0.0)
    return cos, sin, et.reshape(BH, 128, 256), np.eye(128, dtype=np.float32).astype(ml_dtypes.bfloat16)


def _in_maps(inputs, cores):
    cos, sin, et, idn = _consts()
    f = lambda a: np.ascontiguousarray(np.asarray(a, dtype=np.float32))
    shared = {
        "w1g": f(inputs["w1_gate"][0]), "w1u": f(inputs["w1_up"][0]), "w1d": f(inputs["w1_down"][0]),
        "win": f(inputs["w_in"][0]), "wba": f(inputs["w_branch_a"][0]), "wbb": f(inputs["w_branch_b"][0]),
        "wo": f(inputs["w_out"][0]), "w2g": f(inputs["w2_gate"][0]), "w2u": f(inputs["w2_up"][0]),
        "w2d": f(inputs["w2_down"][0]),
        "g1": f(inputs["g_ffn1"]).reshape(1, D), "gm": f(inputs["g_mix"]).reshape(1, D),
        "g2": f(inputs["g_ffn2"]).reshape(1, D), "gf": f(inputs["g_final"]).reshape(1, D),
        "qn": f(inputs["q_norm_a"]).reshape(1, HD), "kn": f(inputs["k_norm_a"]).reshape(1, HD),
        "etab": et, "idn": idn,
    }
    x = np.asarray(inputs["x"], dtype=np.float32)
    maps = []
    for c in cores:
        b, half = c // 2, c % 2
        m = dict(shared)
        if half == 0:
            m["x"] = np.ascontiguousarray(x[b])
            m["cosr"], m["sinr"] = cos, sin
        else:
            m["x"] = np.ascontiguousarray(x[b, ::-1])
            m["cosr"], m["sinr"] = np.ascontiguousarray(cos[::-1]), np.ascontiguousarray(sin[::-1])
        maps.append(m)
    return maps


def kernel(**inputs):
    nc = build()
    cores = list(range(8))
    res = run_bass_kernel_spmd(nc, _in_maps(inputs, cores), core_ids=cores)
    out = np.empty((NB, T, D), np.float32)
    for c in cores:
        b, half = c // 2, c % 2
        o = res.results[c]["out"]
        if half == 0:
            out[b, 0:OWN] = o
        else:
            out[b, OWN:T] = o[::-1]
    return out
```

```python
import math
import numpy as np
import ml_dtypes
import concourse.bass as bass
import concourse.mybir as mybir
from concourse.bass_utils import run_bass_kernel_spmd

F32 = mybir.dt.float32
BF16 = mybir.dt.bfloat16
AF = mybir.ActivationFunctionType
ALU = mybir.AluOpType
AX = mybir.AxisListType

D = 4096
T = 4096
NB = 4
DFF = 11008
HD = 128
AH = 16
AKV = 4
BG = 3
BHG = 8
BH = 24
DIL = (1, 4, 16)
C0 = 2048
C1 = C0 + 512
C2 = C1 + 512
C3 = C2 + 9216
INW = C3 + 2 * D
TT = 512
OWN = 2048
EPS = 1e-6
SCALE = HD ** -0.5
KC_D = D // 128
KC_F = DFF // 128
NSLOT = 3
PREF = 2


class Buf:
    __slots__ = ("name", "w", "r", "chan")

    def __init__(self, name):
        self.name = name
        self.w = {}
        self.r = {}
        self.chan = None


class Prog:
    def __init__(self):
        self.ops = {e: [] for e in ("pe", "act", "dve", "pool", "sp")}
        self.nchan = 0
        self.chan_val = []
        self.waited = {e: {} for e in self.ops}

    def alias(self, new, olds):
        for o in olds:
            for k, v in o.w.items():
                if new.r.get(k, -1) < v:
                    new.r[k] = v
            for k, v in o.r.items():
                if new.r.get(k, -1) < v:
                    new.r[k] = v
        return new

    def op(self, eng, fn, reads=(), writes=(), pwrites=(), dma=0, chan_buf=None):
        waits = {}

        def need(evs):
            for k, v in evs.items():
                if waits.get(k, -1) < v:
                    waits[k] = v

        for b in reads:
            need(b.w)
        for b in writes:
            need(b.w)
            need(b.r)
        for b in pwrites:
            need(b.r)
        if eng == "pe":
            waits.pop("pe", None)
        wl = []
        wd = self.waited[eng]
        for k, v in waits.items():
            if wd.get(k, -1) >= v:
                continue
            wd[k] = v
            wl.append((k, v))
        lst = self.ops[eng]
        seq = len(lst)
        if dma:
            if chan_buf.chan is None:
                chan_buf.chan = self.nchan
                self.nchan += 1
                self.chan_val.append(0)
            c = chan_buf.chan
            self.chan_val[c] += 16 * dma
            ev = (("d", c), self.chan_val[c])
        else:
            ev = (eng, seq)
        lst.append({"fn": fn, "waits": wl, "dma": dma, "chan": chan_buf.chan if dma else None, "sig": False})
        for b in reads:
            if b.r.get(ev[0], -1) < ev[1]:
                b.r[ev[0]] = ev[1]
        for b in writes:
            b.w = {ev[0]: ev[1]}
            b.r = {}
        for b in pwrites:
            if b.w.get(ev[0], -1) < ev[1]:
                b.w[ev[0]] = ev[1]

    def finalize(self):
        for e, lst in self.ops.items():
            for o in lst:
                for k, v in o["waits"]:
                    if not isinstance(k, tuple):
                        self.ops[k][v]["sig"] = True
        self.cnt = {}
        for e, lst in self.ops.items():
            c = 0
            arr = []
            for o in lst:
                if o["sig"] and not o["dma"]:
                    c += 1
                arr.append(c)
            self.cnt[e] = arr

    def replay(self, eng, e, esem, csems):
        for o in self.ops[eng]:
            for k, v in o["waits"]:
                if isinstance(k, tuple):
                    e.wait_ge(csems[k[1]], v)
                else:
                    e.wait_ge(esem[k], self.cnt[k][v])
            r = o["fn"](e)
            if o["dma"]:
                if not isinstance(r, (list, tuple)):
                    r = [r]
                assert len(r) == o["dma"], (len(r), o["dma"])
                for ins in r:
                    ins.then_inc(csems[o["chan"]], 16)
            elif o["sig"]:
                if isinstance(r, (list, tuple)):
                    r = r[-1]
                r.then_inc(esem[eng], 1)


def build(debug=None):
    dbg = debug or {}
    nc = bass.Bass("TRN2", target_bir_lowering=False)
    P = Prog()

    def dram_in(name, shape, dt=F32):
        return nc.dram_tensor(name, list(shape), dt, kind="ExternalInput").ap()

    def dram_tmp(name, shape, dt):
        kind = "ExternalOutput" if name in dbg.get("dump", ()) else "Internal"
        return nc.dram_tensor(name, list(shape), dt, kind=kind).ap()

    x_d = dram_in("x", [T, D])
    wsrc = {
        "w1g": dram_in("w1g", [D, DFF]), "w1u": dram_in("w1u", [D, DFF]), "w1d": dram_in("w1d", [DFF, D]),
        "win": dram_in("win", [D, INW]),
        "wba": dram_in("wba", [2048, D]), "wbb": dram_in("wbb", [1024, D]), "wo": dram_in("wo", [D, D]),
        "w2g": dram_in("w2g", [D, DFF]), "w2u": dram_in("w2u", [D, DFF]), "w2d": dram_in("w2d", [DFF, D]),
    }
    gvec = {k: dram_in(k, [1, D]) for k in ("g1", "gm", "g2", "gf")}
    qn_d = dram_in("qn", [1, HD])
    kn_d = dram_in("kn", [1, HD])
    cos_d = dram_in("cosr", [T, 256])
    sin_d = dram_in("sinr", [T, 256])
    etab_d = dram_in("etab", [BH, 128, 256])
    idn_d = dram_in("idn", [128, 128], BF16)
    out_d = nc.dram_tensor("out", [OWN, D], F32, kind="ExternalOutput").ap()

    wb = {k: dram_tmp(k + "_b", v.shape, BF16) for k, v in wsrc.items()}
    H1 = dram_tmp("H1", [T, D], F32)
    H2 = dram_tmp("H2", [OWN, D], F32)
    H3 = dram_tmp("H3", [OWN, D], F32)
    QA = dram_tmp("QA", [AH, 128, OWN], BF16)
    KA = dram_tmp("KA", [AKV, 128, T], BF16)
    VA = dram_tmp("VA", [T, 512], BF16)
    QB = dram_tmp("QB", [BH, 128, OWN], BF16)
    KB = dram_tmp("KB", [BH, 128, T], BF16)
    VB = dram_tmp("VB", [T, BH * 128], BF16)
    SG = dram_tmp("SG", [2, 32, 128, OWN], BF16)
    YA = dram_tmp("YA", [AH, 128, OWN], BF16)
    YB = dram_tmp("YB", [BHG, 128, OWN], BF16)


    from contextlib import ExitStack
    es = ExitStack()
    E = es.enter_context
    actT = E(nc.sbuf_tensor("actT", [128, KC_D, TT], BF16))
    big = E(nc.sbuf_tensor("big", [128, KC_F * TT], BF16))
    ring = [E(nc.sbuf_tensor(f"ring{i}", [128, 4096], BF16)) for i in range(NSLOT)]
    stg = [E(nc.sbuf_tensor(f"stg{i}", [128, 2048], F32)) for i in range(2)]
    silu_t = E(nc.sbuf_tensor("silu_t", [128, 1024], F32))
    cs_t = E(nc.sbuf_tensor("cs_t", [128, 2, 4, 256], F32))
    ident = E(nc.sbuf_tensor("ident", [128, 128], BF16))
    ones = E(nc.sbuf_tensor("ones", [128, 128], BF16))
    qkn = E(nc.sbuf_tensor("qkn", [128, 2, 128], F32))
    small = E(nc.sbuf_tensor("small", [128, 64], F32))
    ps = E(nc.psum_tensor("ps", [128, 4096], F32))

    def bank(i):
        return ps[:, i * 512:(i + 1) * 512]

    bankb = [Buf(f"bank{i}") for i in range(8)]
    hT = big[:, :].rearrange("p (c t) -> p c t", t=TT)
    big32 = big[:, :].bitcast(F32)
    xrow = [big32[:, 0:4096], big32[:, 4096:8192], big32[:, 15360:19456]]
    grep = big32[:, 8192:12288]
    junk = big[:, 24576:28672]
    xnp = [big[:, 28672:29696], big[:, 29696:30720]]

    B_hT = Buf("hT")
    B_actT = Buf("actT")
    B_ring = [Buf(f"ring{i}") for i in range(NSLOT)]
    B_stg = [Buf(f"stg{i}") for i in range(2)]
    B_silu = Buf("silu")
    B_cs = Buf("cs")
    B_const = Buf("const")
    B_small = Buf("small")
    B_sm = [Buf(f"sm{i}") for i in range(4)]
    B_xrow = [Buf("xrow0"), Buf("xrow1"), Buf("xrow2")]
    B_grep = Buf("grep")
    B_junk = Buf("junk")
    B_xnp = [Buf("xnp0"), Buf("xnp1")]
    B_w = {k: Buf("w_" + k) for k in wsrc}
    B_x = Buf("x_in")
    B_H1 = [Buf(f"H1_{t}") for t in range(8)]
    B_H2 = [Buf(f"H2_{t}") for t in range(4)]
    B_H3 = [Buf(f"H3_{t}") for t in range(4)]
    B_QA, B_KA, B_VA, B_QB, B_KB, B_VB, B_SG, B_YA, B_YB = (Buf(n) for n in
                                                           ("QA", "KA", "VA", "QB", "KB", "VB", "SG", "YA", "YB"))
    B_out = Buf("out")
    rr = {"bank": 0, "stg": 0, "f32": 0, "ce": 0}
    f32s = [E(nc.sbuf_tensor(f"f32s{i}", [128, 2048], F32)) for i in range(2)]
    B_f32s = [Buf("f32s0"), Buf("f32s1")]

    def emit_cast(k):
        src, dst = wsrc[k], wb[k]
        R, Cc = src.shape
        rows = max(1, min(R, (6 << 20) // (Cc * 4)))
        r0 = 0
        while r0 < R:
            r1 = min(R, r0 + rows)
            P.op("pool", lambda e, a=r0, b=r1, s=src, d=dst: e.dma_start(out=d[a:b, :], in_=s[a:b, :]),
                 pwrites=[B_w[k]], dma=1, chan_buf=B_w[k])
            r0 = r1

    bg_units = []
    for k in ("wba", "wbb", "wo", "w2g", "w2u", "w2d"):
        R_, C_ = wsrc[k].shape
        for r0 in range(0, R_, 128):
            for c0 in range(0, C_, 2048):
                bg_units.append((k, r0, c0, min(2048, C_ - c0)))
    bg_state = {"on": False, "cnt": 0, "pend": None}

    def bg_tick():
        if not bg_state["on"]:
            return
        bg_state["cnt"] += 1
        if bg_state["cnt"] % 2:
            return
        if bg_state["pend"] is not None:
            bg_state["pend"]()
            bg_state["pend"] = None
        if not bg_units:
            return
        k, r0, c0, w = bg_units.pop(0)
        fs, B_fs = f32s[0], B_f32s[0]
        os_, B_os = f32s[1][:, :].bitcast(BF16), B_f32s[1]
        P.op("sp", lambda e: e.dma_start(out=fs[:, 0:w], in_=wsrc[k][r0:r0 + 128, c0:c0 + w]),
             writes=[B_fs], dma=1, chan_buf=B_fs)
        P.op("pool", lambda e: e.tensor_copy(out=os_[:, 0:w], in_=fs[:, 0:w]), reads=[B_fs], writes=[B_os])

        def store():
            P.op("sp", lambda e: e.dma_start(out=wb[k][r0:r0 + 128, c0:c0 + w], in_=os_[:, 0:w]),
                 reads=[B_os], pwrites=[B_w[k]], dma=1, chan_buf=B_os)
        bg_state["pend"] = store

    P.op("sp", lambda e: [e.dma_start(out=ident[:], in_=idn_d[:]),
                          e.dma_start(out=qkn[:, 0, :], in_=qn_d[0:1, :].partition_broadcast(128)),
                          e.dma_start(out=qkn[:, 1, :], in_=kn_d[0:1, :].partition_broadcast(128))],
         writes=[B_const], dma=3, chan_buf=B_const)
    B_ones = Buf("ones")
    P.op("dve", lambda e: e.memset(ones[:], 1.0), writes=[B_ones])

    steps = []

    def run_steps():
        n = len(steps)
        chunk_ids = [i for i in range(n) if steps[i][0] is not None]
        slot_of = {ci: j % NSLOT for j, ci in enumerate(chunk_ids)}
        issued = 0
        pending_st = []

        def issue(j):
            ci = chunk_ids[j]
            dfn, wkeys, ndma, _ = steps[ci]
            sl = slot_of[ci]
            while pending_st:
                pending_st.pop(0)()
            if isinstance(dfn, tuple):
                dfn[1](ring[sl], B_ring[sl], pending_st)
                return
            P.op("sp", lambda e, f=dfn, s=sl: f(e, ring[s]), reads=[B_w[k] for k in wkeys],
                 writes=[B_ring[sl]], dma=ndma, chan_buf=B_ring[sl])

        pos = {ci: j for j, ci in enumerate(chunk_ids)}
        for i in range(n):
            if steps[i][0] is not None:
                j = pos[i]
                while issued <= min(j + PREF, len(chunk_ids) - 1):
                    issue(issued)
                    issued += 1
                steps[i][3](ring[slot_of[i]], B_ring[slot_of[i]])
                bg_tick()
            else:
                steps[i][3](None, None)
        while pending_st:
            pending_st.pop(0)()
        assert not bg_units, len(bg_units)
        if bg_state["pend"] is not None:
            bg_state["pend"]()
            bg_state["pend"] = None
        steps.clear()

    def add_step(body, dfn=None, wkeys=(), ndma=1):
        steps.append((dfn, wkeys, ndma, body))

    def gemm(kind, src, B_src, KC, slab_list, epilogue, fu=False):
        pend = [None]
        deferred = []
        for slab in slab_list:
            pset = rr["bank"]
            rr["bank"] ^= 1
            banks = [pset * 4 + a for a in range(4)]
            ngrp = (KC + 7) // 8
            for kg in range(ngrp):
                k0 = kg * 8
                kn = min(8, KC - k0)
                if kind == "FM":
                    def dfn(e, slot, slab=slab, k0=k0, kn=kn):
                        sv = slot[:, :].rearrange("p (k c) -> p k c", c=512)
                        res = []
                        a = 0
                        while a < 4:
                            b = a
                            while b + 1 < 4 and slab[b + 1][0] == slab[a][0] and slab[b + 1][1] == slab[b][1] + 128:
                                b += 1
                            wkey, c0 = slab[a]
                            wsrc_ap = wb[wkey][k0 * 128:(k0 + kn) * 128, c0:c0 + 128 * (b - a + 1)]
                            res.append(e.dma_start(out=sv[:, 0:kn, a * 128:(b + 1) * 128],
                                                   in_=wsrc_ap.rearrange("(k p) c -> p k c", p=128)))
                            a = b + 1
                        return res
                    nd = 0
                    a = 0
                    while a < 4:
                        b = a
                        while b + 1 < 4 and slab[b + 1][0] == slab[a][0] and slab[b + 1][1] == slab[b][1] + 128:
                            b += 1
                        nd += 1
                        a = b + 1
                    wkeys = sorted(set(s[0] for s in slab))
                else:
                    def dfn(e, slot, slab=slab, k0=k0, kn=kn):
                        sv = slot[:, :].rearrange("p (k c) -> p k c", c=512)
                        wkey, c0 = slab
                        return [e.dma_start(out=sv[:, 0:kn, :],
                                            in_=wb[wkey][k0 * 128:(k0 + kn) * 128, c0:c0 + 512].rearrange(
                                                "(k p) c -> p k c", p=128))]
                    nd = 1
                    wkeys = [slab[0]]

                def body(slot, B_slot, k0=k0, kn=kn, banks=banks, KC=KC):
                    sv = slot[:, :].rearrange("p (k c) -> p k c", c=512)
                    for kk in range(kn):
                        kc = k0 + kk
                        for a in range(4):
                            if kind == "FM":
                                lhsT, rhs = sv[:, kk, a * 128:(a + 1) * 128], src[:, kc, :]
                            else:
                                lhsT, rhs = src[:, kc, a * 128:(a + 1) * 128], sv[:, kk, :]
                            P.op("pe", lambda e, o=bank(banks[a]), l=lhsT, r=rhs, st=(kc == 0), sp=(kc == KC - 1):
                                 e.matmul(o, l, r, start=st, stop=sp),
                                 reads=[B_slot, B_src], writes=[bankb[banks[a]]])
                if fu:
                    if kind == "FM":
                        parts = []
                        a = 0
                        while a < 4:
                            b = a
                            while b + 1 < 4 and slab[b + 1][0] == slab[a][0] and slab[b + 1][1] == slab[b][1] + 128:
                                b += 1
                            parts.append((slab[a][0], slab[a][1], a * 128, (b - a + 1) * 128))
                            a = b + 1
                    else:
                        parts = [(slab[0], slab[1], 0, 512)]

                    def fu_issue(slot, B_slot, pending_st, parts=parts, k0=k0, kn=kn):
                        sv = slot[:, :].rearrange("p (k c) -> p k c", c=512)
                        for hf in range((kn + 3) // 4):
                            ka = hf * 4
                            kb = min(kn, ka + 4)
                            fi = rr["f32"]
                            rr["f32"] ^= 1
                            fs, B_fs = f32s[fi], B_f32s[fi]
                            fv = fs[:, :].rearrange("p (k c) -> p k c", c=512)
                            P.op("sp", lambda e, ka=ka, kb=kb, fv=fv: [
                                e.dma_start(out=fv[:, 0:kb - ka, so:so + w],
                                            in_=wsrc[wk][(k0 + ka) * 128:(k0 + kb) * 128, c0:c0 + w].rearrange(
                                                "(k p) c -> p k c", p=128)) for (wk, c0, so, w) in parts],
                                 writes=[B_fs], dma=len(parts), chan_buf=B_fs)
                            ce = "dve" if rr["ce"] == 0 else "act"
                            rr["ce"] ^= 1
                            wr = dict(writes=[B_slot]) if hf == 0 else dict(pwrites=[B_slot])
                            if ce == "dve":
                                P.op("dve", lambda e, ka=ka, kb=kb, fv=fv: e.tensor_copy(out=sv[:, ka:kb, :], in_=fv[:, 0:kb - ka, :]),
                                     reads=[B_fs], **wr)
                            else:
                                P.op("act", lambda e, ka=ka, kb=kb, fv=fv: e.activation(out=sv[:, ka:kb, :], in_=fv[:, 0:kb - ka, :], func=AF.Copy),
                                     reads=[B_fs], **wr)

                        def store():
                            P.op("sp", lambda e: [
                                e.dma_start(out=wb[wk][k0 * 128:(k0 + kn) * 128, c0:c0 + w].rearrange("(k p) c -> p k c", p=128),
                                            in_=sv[:, 0:kn, so:so + w]) for (wk, c0, so, w) in parts],
                                 reads=[B_slot], pwrites=[B_w[wk] for (wk, _, _, _) in parts], dma=len(parts), chan_buf=B_slot)
                        pending_st.append(store)
                    add_step(body, (None, fu_issue), wkeys, nd)
                else:
                    add_step(body, dfn, wkeys, nd)
            if pend[0] is not None:
                add_step(pend[0])
                pend[0] = None

            def ep_step(slot, B_slot, slab=slab, banks=banks):
                d = epilogue(slab, banks)
                if d is not None:
                    deferred.append(d)
            add_step(ep_step)

            def pend_step(slot, B_slot):
                while deferred:
                    deferred.pop(0)()
            pend[0] = pend_step
        if pend[0] is not None:
            add_step(pend[0])

    def norm_tile(src_d, row0, B_src, gkey, final_out=None):
        add_step(lambda slot, B_slot: _norm_tile(src_d, row0, B_src, gkey, final_out))

    def _norm_tile(src_d, row0, B_src, gkey, final_out=None):
        for b in B_xrow + [B_grep, B_junk] + B_xnp:
            P.alias(b, [B_hT])
        P.op("sp", lambda e: e.dma_start(out=grep, in_=gvec[gkey][0:1, :].partition_broadcast(128)),
             writes=[B_grep], dma=1, chan_buf=B_grep)
        for s in range(4):
            xr, B_xr = xrow[s % 3], B_xrow[s % 3]
            r0 = row0 + s * 128
            P.op("sp", lambda e, xr=xr, r0=r0: e.dma_start(out=xr, in_=src_d[r0:r0 + 128, :]),
                 reads=[B_src], writes=[B_xr], dma=1, chan_buf=B_xr)
            ssc = small[:, s:s + 1]
            sdc = small[:, 8 + s:9 + s]
            rsc = small[:, 16 + s:17 + s]
            P.op("act", lambda e, xr=xr, ssc=ssc: e.activation(out=junk, in_=xr, func=AF.Square, scale=float(D ** -0.5),
                                                              accum_out=ssc),
                 reads=[B_xr], writes=[B_junk, B_sm[s]])
            P.op("act", lambda e, ssc=ssc, sdc=sdc: e.activation(out=sdc, in_=ssc, func=AF.Sqrt, bias=EPS, scale=1.0),
                 reads=[B_sm[s]], writes=[B_sm[s]])
            P.op("dve", lambda e, sdc=sdc, rsc=rsc: e.reciprocal(out=rsc, in_=sdc), reads=[B_sm[s]], writes=[B_sm[s]])
            if final_out is not None:
                P.op("dve", lambda e, xr=xr, rsc=rsc: e.scalar_tensor_tensor(out=xr, in0=xr, scalar=rsc, in1=grep,
                                                                            op0=ALU.mult, op1=ALU.mult),
                     reads=[B_sm[s], B_grep], writes=[B_xr])
                P.op("sp", lambda e, xr=xr, r0=r0: e.dma_start(out=final_out[r0:r0 + 128, :], in_=xr),
                     reads=[B_xr], pwrites=[B_out], dma=1, chan_buf=B_xr)
                continue
            for pc in range(4):
                xp, B_xp = xnp[pc % 2], B_xnp[pc % 2]
                P.op("dve", lambda e, xr=xr, rsc=rsc, xp=xp, pc=pc: e.scalar_tensor_tensor(
                    out=xp, in0=xr[:, pc * 1024:(pc + 1) * 1024], scalar=rsc, in1=grep[:, pc * 1024:(pc + 1) * 1024],
                    op0=ALU.mult, op1=ALU.mult), reads=[B_xr, B_sm[s], B_grep], writes=[B_xp])
                bk = rr["bank"] * 4 + (pc % 4)
                pb = bank(bk).bitcast(BF16)
                for c in range(8):
                    P.op("pe", lambda e, o=pb[:, c * 128:(c + 1) * 128], i=xp[:, c * 128:(c + 1) * 128]:
                         e.transpose(o, i, ident[:]), reads=[B_xp, B_const], writes=[bankb[bk]])
                eng = "act" if pc % 2 == 0 else "dve"
                dst = actT[:, pc * 8:(pc + 1) * 8, s * 128:(s + 1) * 128]
                srcp = pb.rearrange("p (c t) -> p c t", t=128)
                if eng == "act":
                    P.op("act", lambda e, d=dst, s_=srcp: e.activation(out=d, in_=s_, func=AF.Copy),
                         reads=[bankb[bk]], pwrites=[B_actT])
                else:
                    P.op("dve", lambda e, d=dst, s_=srcp: e.tensor_copy(out=d, in_=s_),
                         reads=[bankb[bk]], pwrites=[B_actT])
            rr["bank"] ^= 1
        P.alias(B_hT, B_xrow + [B_grep, B_junk] + B_xnp)

    def ffn_tile(src_d, row0, B_src, gkey, wg, wu, wd, dst_d, drow0, B_dst, fu=False):
        norm_tile(src_d, row0, B_src, gkey)

        def ep_gu(slab, banks):
            f0 = slab[0][1] // 128
            nvalid = 2 if f0 + 2 <= KC_F else 1
            g_ps = ps[:, banks[0] * 512: banks[0] * 512 + 1024]
            u_ps = ps[:, banks[2] * 512: banks[2] * 512 + 1024]
            P.op("act", lambda e: e.activation(out=silu_t[:], in_=g_ps, func=AF.Silu),
                 reads=[bankb[banks[0]], bankb[banks[1]]], writes=[B_silu])
            P.op("dve", lambda e: e.tensor_tensor(out=hT[:, f0:f0 + 2, :].rearrange("p c t -> p (c t)"), in0=silu_t[:],
                                                  in1=u_ps, op=ALU.mult),
                 reads=[B_silu, bankb[banks[2]], bankb[banks[3]]], pwrites=[B_hT])

        slabs = []
        for f0 in range(0, KC_F, 2):
            c = f0 * 128
            slabs.append([(wg, c), (wg, c + 128), (wu, c), (wu, c + 128)])
        gemm("FM", actT, B_actT, KC_D, slabs, ep_gu, fu=fu)

        def ep_down(slab, banks):
            c0 = slab[1]
            si = rr["stg"]
            rr["stg"] ^= 1
            st, B_st = stg[si], B_stg[si]
            stv = st[:, :].rearrange("p (s c) -> p s c", c=512)
            P.op("sp", lambda e: e.dma_start(out=stv, in_=src_d[row0:row0 + 512, c0:c0 + 512].rearrange(
                "(s p) c -> p s c", p=128)), reads=[B_src], writes=[B_st], dma=1, chan_buf=B_st)
            pss = ps[:, banks[0] * 512: banks[0] * 512 + 2048]
            P.op("dve", lambda e: e.scalar_tensor_tensor(out=st[:, :], in0=pss, scalar=0.5, in1=st[:, :],
                                                         op0=ALU.mult, op1=ALU.add),
                 reads=[bankb[b] for b in banks], writes=[B_st])
            P.op("sp", lambda e: e.dma_start(out=dst_d[drow0:drow0 + 512, c0:c0 + 512].rearrange(
                "(s p) c -> p s c", p=128), in_=stv), reads=[B_st], pwrites=[B_dst], dma=1, chan_buf=B_st)

        gemm("TM", hT, B_hT, KC_F, [(wd, c) for c in range(0, D, 512)], ep_down, fu=fu)

    qrot = E(nc.sbuf_tensor("qrot", [128, 2048], BF16))
    tstage = [E(nc.sbuf_tensor(f"tstage{i}", [128, 2048], BF16)) for i in range(2)]
    B_qrot = Buf("qrot")
    B_tst = [Buf("tst0"), Buf("tst1")]
    rr["tst"] = 0

    def next_tst():
        i = rr["tst"]
        rr["tst"] ^= 1
        return tstage[i], B_tst[i]

    def ps4(banks):
        return ps[:, banks[0] * 512: banks[0] * 512 + 2048]

    def inproj_tile(t):
        own = t < 4
        needB = t < 6
        row0 = t * TT
        norm_tile(H1, row0, B_H1[t], "gm")

        def ld_cs(slot, B_slot):
            P.op("sp", lambda e: [e.dma_start(out=cs_t[:, 0], in_=cos_d[row0:row0 + 512, :].rearrange("(s p) c -> p s c", p=128)),
                                  e.dma_start(out=cs_t[:, 1], in_=sin_d[row0:row0 + 512, :].rearrange("(s p) c -> p s c", p=128))],
                 writes=[B_cs], dma=2, chan_buf=B_cs)
        add_step(ld_cs)

        def ep_qk(which, head0, dst, B_dst):
            def ep(slab, banks):
                pv = ps4(banks)
                pv3 = pv.rearrange("p (a d) -> p a d", d=128)
                sA, B_sA = stg[0], B_stg[0]
                sB, B_sB = stg[1], B_stg[1]
                sA3 = sA[:, :].rearrange("p (a d) -> p a d", d=128)
                ssq = small[:, 24:40]
                P.op("act", lambda e: e.activation(out=sA[:, :], in_=pv, func=AF.Square, scale=float(HD ** -0.5)),
                     reads=[bankb[b] for b in banks], writes=[B_sA])
                P.op("dve", lambda e: e.tensor_reduce(out=ssq, in_=sA3, axis=AX.X, op=ALU.add),
                     reads=[B_sA], writes=[B_small])
                P.op("act", lambda e: e.activation(out=ssq, in_=ssq, func=AF.Sqrt, bias=EPS, scale=1.0),
                     reads=[B_small], writes=[B_small])
                P.op("dve", lambda e: e.reciprocal(out=ssq, in_=ssq), reads=[B_small], writes=[B_small])
                P.op("dve", lambda e: e.tensor_tensor(out=sA3, in0=pv3, in1=ssq.unsqueeze(2).to_broadcast([128, 16, 128]),
                                                      op=ALU.mult),
                     reads=[bankb[b] for b in banks] + [B_small], writes=[B_sA])
                P.op("dve", lambda e: e.tensor_tensor(out=sA3, in0=sA3,
                                                      in1=qkn[:, which:which + 1, :].to_broadcast([128, 16, 128]),
                                                      op=ALU.mult), reads=[B_const], writes=[B_sA])
                x5 = sA[:, :].rearrange("p (s h two i) -> p s h two i", s=4, h=8, two=2, i=32)
                x1, x2 = x5[:, :, :, 0, :], x5[:, :, :, 1, :]
                o5 = qrot[:, :].rearrange("p (s h two i) -> p s h two i", s=4, h=8, two=2, i=32)
                o1, o2 = o5[:, :, :, 0, :], o5[:, :, :, 1, :]
                Cv = cs_t[:, 0].rearrange("p s (h i) -> p s h i", i=32)
                Sv = cs_t[:, 1].rearrange("p s (h i) -> p s h i", i=32)
                tA = sB[:, 0:1024].rearrange("p (s h i) -> p s h i", s=4, h=8)
                tB = sB[:, 1024:2048].rearrange("p (s h i) -> p s h i", s=4, h=8)
                P.op("dve", lambda e: e.tensor_tensor(out=tA, in0=x1, in1=Cv, op=ALU.mult), reads=[B_sA, B_cs], writes=[B_sB])
                P.op("dve", lambda e: e.tensor_tensor(out=tB, in0=x2, in1=Sv, op=ALU.mult), reads=[B_sA, B_cs], writes=[B_sB])
                P.op("dve", lambda e: e.tensor_tensor(out=o1, in0=tA, in1=tB, op=ALU.subtract), reads=[B_sB], writes=[B_qrot])
                P.op("dve", lambda e: e.tensor_tensor(out=tA, in0=x1, in1=Sv, op=ALU.mult), reads=[B_sA, B_cs], writes=[B_sB])
                P.op("dve", lambda e: e.tensor_tensor(out=tB, in0=x2, in1=Cv, op=ALU.mult), reads=[B_sA, B_cs], writes=[B_sB])
                P.op("dve", lambda e: e.tensor_tensor(out=o2, in0=tA, in1=tB, op=ALU.add), reads=[B_sB, B_qrot], writes=[B_qrot])

                def deferred():
                    pT = ps[:, banks[0] * 512: banks[0] * 512 + 1024].bitcast(BF16)
                    for hh in range(4):
                        for a in range(4):
                            P.op("pe", lambda e, o=pT[:, hh * 512 + a * 128: hh * 512 + (a + 1) * 128],
                                 i=qrot[:, (a * 4 + hh) * 128:(a * 4 + hh + 1) * 128]: e.transpose(o, i, ident[:]),
                                 reads=[B_qrot, B_const], writes=[bankb[banks[0]], bankb[banks[1]]])
                    ts, B_ts = next_tst()
                    P.op("act", lambda e: e.activation(out=ts[:, :], in_=pT, func=AF.Copy),
                         reads=[bankb[banks[0]], bankb[banks[1]]], writes=[B_ts])
                    P.op("sp", lambda e: e.dma_start(
                        out=dst[head0:head0 + 4, :, row0:row0 + 512].rearrange("h p c -> p h c"),
                        in_=ts[:, :].rearrange("p (h c) -> p h c", c=512)), reads=[B_ts], pwrites=[B_dst], dma=1, chan_buf=B_ts)
                return deferred
            return ep

        def ep_v(dst, dcol, B_dst):
            def ep(slab, banks):
                ts, B_ts = next_tst()
                P.op("act", lambda e: e.activation(out=ts[:, :], in_=ps4(banks), func=AF.Copy),
                     reads=[bankb[b] for b in banks], writes=[B_ts])
                P.op("sp", lambda e: e.dma_start(
                    out=dst[row0:row0 + 512, dcol:dcol + 512].rearrange("(s p) c -> p s c", p=128),
                    in_=ts[:, :].rearrange("p (s c) -> p s c", c=512)), reads=[B_ts], pwrites=[B_dst], dma=1, chan_buf=B_ts)
            return ep

        def ep_fm(dst3, B_dst, func):
            def ep(slab, banks):
                ts, B_ts = next_tst()
                P.op("act", lambda e: e.activation(out=ts[:, :], in_=ps4(banks), func=func),
                     reads=[bankb[b] for b in banks], writes=[B_ts])
                P.op("sp", lambda e: e.dma_start(out=dst3.rearrange("h p c -> p h c"),
                                                 in_=ts[:, :].rearrange("p (h c) -> p h c", c=512)),
                     reads=[B_ts], pwrites=[B_dst], dma=1, chan_buf=B_ts)
            return ep

        if own:
            for gi in range(4):
                gemm("TM", actT, B_actT, KC_D, [("win", gi * 512)], ep_qk(0, gi * 4, QA, B_QA), fu=(t == 0))
        gemm("TM", actT, B_actT, KC_D, [("win", C0)], ep_qk(1, 0, KA, B_KA), fu=(t == 0))
        gemm("TM", actT, B_actT, KC_D, [("win", C1)], ep_v(VA, 0, B_VA), fu=(t == 0))
        if needB:
            for g in range(BG):
                for hh in range(2):
                    gemm("TM", actT, B_actT, KC_D, [("win", C2 + (g * 3 + 2) * 1024 + hh * 512)],
                         ep_v(VB, (g * 8 + hh * 4) * 128, B_VB), fu=(t == 0))
        def fm_slab(c0):
            return [("win", c0 + a * 128) for a in range(4)]
        if needB:
            for g in range(BG):
                for hh in range(2):
                    h0 = g * 8 + hh * 4
                    gemm("FM", actT, B_actT, KC_D, [fm_slab(C2 + (g * 3 + 1) * 1024 + hh * 512)],
                         ep_fm(KB[h0:h0 + 4, :, row0:row0 + 512], B_KB, AF.Copy), fu=(t == 0))
        if own:
            for g in range(BG):
                for hh in range(2):
                    h0 = g * 8 + hh * 4
                    gemm("FM", actT, B_actT, KC_D, [fm_slab(C2 + (g * 3 + 0) * 1024 + hh * 512)],
                         ep_fm(QB[h0:h0 + 4, :, row0:row0 + 512], B_QB, AF.Copy), fu=(t == 0))
            for which in range(2):
                for j in range(8):
                    gemm("FM", actT, B_actT, KC_D, [fm_slab(C3 + which * D + j * 512)],
                         ep_fm(SG[which, j * 4:(j + 1) * 4, :, row0:row0 + 512], B_SG, AF.Sigmoid), fu=(t == 0))

    def attention():
        kT = big[:, 0:4096]
        vsb = big[:, 4096:8192].rearrange("p (b d) -> p b d", d=128)
        qT = [big[:, 8192:8704], big[:, 8704:9216]]
        pT = [big[:, 9216 + i * 512: 9216 + (i + 1) * 512] for i in range(3)]
        yT = [big[:, 10752:11264], big[:, 11264:11776]]
        rden = big32[:, 5888:6400]
        B_kT, B_vsb = Buf("kT"), Buf("vsb")
        B_qT = [Buf("qT0"), Buf("qT1")]
        B_pT = [Buf(f"pT{i}") for i in range(3)]
        B_yT = [Buf("yT0"), Buf("yT1")]
        B_rden = Buf("rden")
        allb = [B_kT, B_vsb, B_rden] + B_qT + B_pT + B_yT
        acc = big32[:, 8192:12288].rearrange("p (two t) -> p two t", two=2)
        qTb = big[:, 24576:26624]
        kTb = big[:, 26624:30720]
        etb = big32[:, 15360:15616]
        vb = big[:, 31232:33536].rearrange("p (b d) -> p b d", d=128)
        ptmp = big32[:, 16768:17024]
        pm = [big[:, 34048:34304], big[:, 34304:34560]]
        rdb = big32[:, 17280:19328]
        ybT = big[:, 38656:40704]
        B_acc, B_qTb, B_kTb, B_etb, B_vb, B_ptmp, B_rdb, B_ybT = (Buf(n) for n in
                                                                 ("acc", "qTb", "kTb", "etb", "vb", "ptmp", "rdb", "ybT"))
        B_pm = [Buf("pm0"), Buf("pm1")]
        allb += [B_acc, B_qTb, B_kTb, B_etb, B_vb, B_ptmp, B_rdb, B_ybT] + B_pm
        for b in allb:
            P.alias(b, [B_hT])

        pT4 = pT + [big[:, 11776 + 1024:11776 + 1536]]
        B_pT4 = B_pT + [Buf("pT3")]
        P.alias(B_pT4[3], [B_hT])
        allb.append(B_pT4[3])
        units = [(kvh, g, qt) for kvh in range(AKV) for g in range(4) for qt in range(4)]

        def load_q(u):
            kvh, g, qt = units[u]
            head = kvh * 4 + g
            qi = u % 2
            P.op("sp", lambda e: e.dma_start(out=qT[qi], in_=QA[head, :, qt * 512:(qt + 1) * 512]),
                 reads=[B_QA], writes=[B_qT[qi]], dma=1, chan_buf=B_qT[qi])
        load_q(0)
        for u, (kvh, g, qt) in enumerate(units):
            if g == 0 and qt == 0:
                P.op("sp", lambda e, kvh=kvh: e.dma_start(out=kT, in_=KA[kvh]), reads=[B_KA], writes=[B_kT], dma=1, chan_buf=B_kT)
                P.op("sp", lambda e, kvh=kvh: e.dma_start(out=vsb, in_=VA[:, kvh * 128:(kvh + 1) * 128].rearrange(
                    "(b p) d -> p b d", p=128)), reads=[B_VA], writes=[B_vsb], dma=1, chan_buf=B_vsb)
            head = kvh * 4 + g
            qi = u % 2
            nb_, db_ = 4 + 2 * qi, 5 + 2 * qi

            def s_mm(blk, qi=qi):
                sb = blk % 4
                P.op("pe", lambda e: e.matmul(bank(sb), kT[:, blk * 128:(blk + 1) * 128], qT[qi], start=True, stop=True),
                     reads=[B_kT, B_qT[qi]], writes=[bankb[sb]])
            s_mm(0)
            s_mm(1)
            if u + 1 < len(units):
                load_q(u + 1)
            for blk in range(32):
                if blk + 2 < 32:
                    s_mm(blk + 2)
                sb = blk % 4
                P.op("act", lambda e, sb=sb: e.activation(out=pT4[sb], in_=bank(sb), func=AF.Exp, scale=float(SCALE)),
                     reads=[bankb[sb]], writes=[B_pT4[sb]])
                P.op("pe", lambda e, sb=sb, blk=blk, nb_=nb_: e.matmul(bank(nb_), vsb[:, blk, :], pT4[sb], start=(blk == 0), stop=(blk == 31)),
                     reads=[B_vsb, B_pT4[sb]], writes=[bankb[nb_]])
                P.op("pe", lambda e, sb=sb, blk=blk, db_=db_: e.matmul(bank(db_), ones[:], pT4[sb], start=(blk == 0), stop=(blk == 31)),
                     reads=[B_ones, B_pT4[sb]], writes=[bankb[db_]])
            P.op("dve", lambda e, db_=db_: e.reciprocal(out=rden, in_=bank(db_)), reads=[bankb[db_]], writes=[B_rden])
            P.op("dve", lambda e, nb_=nb_, qi=qi: e.tensor_tensor(out=yT[qi], in0=bank(nb_), in1=rden, op=ALU.mult),
                 reads=[bankb[nb_], B_rden], writes=[B_yT[qi]])
            P.op("sp", lambda e, qi=qi, head=head, qt=qt: e.dma_start(out=YA[head, :, qt * 512:(qt + 1) * 512], in_=yT[qi]),
                 reads=[B_yT[qi]], pwrites=[B_YA], dma=1, chan_buf=B_yT[qi])

        ucnt = 0
        for i in range(BHG):
            for g in range(BG):
                h = g * 8 + i
                Dd = DIL[g]
                ML = OWN // Dd
                nblk = ML // 128 + 1
                pad = 64 * Dd
                P.op("sp", lambda e, h=h: e.dma_start(out=qTb, in_=QB[h]), reads=[B_QB], writes=[B_qTb], dma=1, chan_buf=B_qTb)
                P.op("dve", lambda e, pad=pad: e.memset(kTb[:, 0:pad], 0.0), writes=[B_kTb])
                P.op("sp", lambda e, h=h, pad=pad: e.dma_start(out=kTb[:, pad:pad + OWN + pad], in_=KB[h, :, 0:OWN + pad]),
                     reads=[B_KB], writes=[B_kTb], dma=1, chan_buf=B_kTb)
                P.op("sp", lambda e, h=h: e.dma_start(out=etb, in_=etab_d[h]), writes=[B_etb], dma=1, chan_buf=B_etb)
                for r in range(Dd):
                    base = r + pad
                    P.op("sp", lambda e, r=r, Dd=Dd, h=h, base=base, nblk=nblk: [
                        e.dma_start(out=vb[64:128, 0, :], in_=VB[r: r + 64 * Dd: Dd, h * 128:(h + 1) * 128]),
                        e.dma_start(out=vb[:, 1:nblk, :],
                                    in_=VB[base: base + Dd * 128 * (nblk - 1): Dd, h * 128:(h + 1) * 128].rearrange(
                                        "(j p) d -> p j d", p=128))],
                         reads=[B_VB], writes=[B_vb], dma=2, chan_buf=B_vb)
                    for qq in range(ML // 128):
                        m0 = 128 * qq
                        sbk = ucnt % 2
                        ndk = 2 + ucnt % 2
                        pmi = ucnt % 2
                        ucnt += 1
                        qsl = slice(r + Dd * m0, r + Dd * (m0 + 127) + 1, Dd)
                        for jj in range(2):
                            j = qq + jj
                            ksl = slice(r + 128 * Dd * j, r + 128 * Dd * j + Dd * 127 + 1, Dd)
                            P.op("pe", lambda e, sbk=sbk, jj=jj, ksl=ksl, qsl=qsl: e.matmul(
                                bank(sbk)[:, jj * 128:(jj + 1) * 128], kTb[:, ksl], qTb[:, qsl], start=True, stop=True),
                                reads=[B_kTb, B_qTb], writes=[bankb[sbk]])
                        P.op("act", lambda e, sbk=sbk: e.activation(out=ptmp, in_=bank(sbk)[:, 0:256], func=AF.Exp, scale=float(SCALE)),
                             reads=[bankb[sbk]], writes=[B_ptmp])
                        P.op("dve", lambda e, pmi=pmi: e.tensor_tensor(out=pm[pmi], in0=ptmp, in1=etb, op=ALU.mult),
                             reads=[B_ptmp, B_etb], writes=[B_pm[pmi]])
                        for jj in range(2):
                            j = qq + jj
                            lo = 64 if (j == 0) else 0
                            P.op("pe", lambda e, ndk=ndk, jj=jj, j=j, lo=lo, pmi=pmi: e.matmul(
                                bank(ndk)[:, 0:128], vb[lo:128, j, :], pm[pmi][lo:128, jj * 128:(jj + 1) * 128],
                                start=(jj == 0), stop=(jj == 1)), reads=[B_vb, B_pm[pmi]], writes=[bankb[ndk]])
                        for jj in range(2):
                            j = qq + jj
                            lo = 64 if (j == 0) else 0
                            P.op("pe", lambda e, ndk=ndk, jj=jj, lo=lo, pmi=pmi: e.matmul(
                                bank(ndk)[:, 128:256], ones[lo:128, :], pm[pmi][lo:128, jj * 128:(jj + 1) * 128],
                                start=(jj == 0), stop=(jj == 1)), reads=[B_ones, B_pm[pmi]], writes=[bankb[ndk]])
                        ndv = bank(ndk)[:, 0:256].rearrange("p (two q) -> p two q", two=2)
                        if g == 0:
                            P.op("dve", lambda e, ndv=ndv, qsl=qsl: e.tensor_copy(out=acc[:, :, qsl], in_=ndv),
                                 reads=[bankb[ndk]], pwrites=[B_acc])
                        else:
                            P.op("dve", lambda e, ndv=ndv, qsl=qsl: e.tensor_tensor(out=acc[:, :, qsl], in0=acc[:, :, qsl], in1=ndv, op=ALU.add),
                                 reads=[bankb[ndk], B_acc], writes=[B_acc])
            P.op("dve", lambda e: e.reciprocal(out=rdb, in_=acc[:, 1, :]), reads=[B_acc], writes=[B_rdb])
            P.op("dve", lambda e: e.tensor_tensor(out=ybT, in0=acc[:, 0, :], in1=rdb, op=ALU.mult), reads=[B_acc, B_rdb], writes=[B_ybT])
            P.op("sp", lambda e, i=i: e.dma_start(out=YB[i], in_=ybT), reads=[B_ybT], pwrites=[B_YB], dma=1, chan_buf=B_ybT)
        P.alias(B_hT, allb)

    def phase3_tile(t):
        row0 = t * TT
        yaT = big[:, 0:8192].rearrange("p (h c) -> p h c", c=512)
        ybT3 = big[:, 8192:12288].rearrange("p (h c) -> p h c", c=512)
        sgt = [big[:, 12288 + i * 2048: 12288 + (i + 1) * 2048] for i in range(2)]
        m1 = big32[:, 8192:10240]
        B_ya, B_yb3, B_m1 = Buf("yaT"), Buf("ybT3"), Buf("m1")
        B_sgt = [Buf("sgt0"), Buf("sgt1")]
        allb = [B_ya, B_yb3, B_m1] + B_sgt

        def start(slot, B_slot):
            for b in allb:
                P.alias(b, [B_hT])
            P.op("sp", lambda e: e.dma_start(out=yaT, in_=YA[:, :, row0:row0 + 512].rearrange("h p c -> p h c")),
                 reads=[B_YA], writes=[B_ya], dma=1, chan_buf=B_ya)
            P.op("sp", lambda e: e.dma_start(out=ybT3, in_=YB[:, :, row0:row0 + 512].rearrange("h p c -> p h c")),
                 reads=[B_YB], writes=[B_yb3], dma=1, chan_buf=B_yb3)
        add_step(start)

        for j in range(8):
            def epA(slab, banks, j=j):
                P.op("sp", lambda e: e.dma_start(out=sgt[0].rearrange("p (h c) -> p h c", c=512),
                                                 in_=SG[0, j * 4:(j + 1) * 4, :, row0:row0 + 512].rearrange("h p c -> p h c")),
                     reads=[B_SG], writes=[B_sgt[0]], dma=1, chan_buf=B_sgt[0])
                P.op("dve", lambda e: e.tensor_tensor(out=m1, in0=ps4(banks), in1=sgt[0], op=ALU.mult),
                     reads=[bankb[b] for b in banks] + [B_sgt[0]], writes=[B_m1])

            def epB(slab, banks, j=j):
                P.op("sp", lambda e: e.dma_start(out=sgt[1].rearrange("p (h c) -> p h c", c=512),
                                                 in_=SG[1, j * 4:(j + 1) * 4, :, row0:row0 + 512].rearrange("h p c -> p h c")),
                     reads=[B_SG], writes=[B_sgt[1]], dma=1, chan_buf=B_sgt[1])
                st, B_st = stg[0], B_stg[0]
                P.op("dve", lambda e: e.tensor_tensor(out=st[:, :], in0=ps4(banks), in1=sgt[1], op=ALU.mult),
                     reads=[bankb[b] for b in banks] + [B_sgt[1]], writes=[B_st])
                P.op("pool", lambda e: e.tensor_tensor(out=actT[:, j * 4:(j + 1) * 4, :].rearrange("p c t -> p (c t)"),
                                                       in0=m1, in1=st[:, :], op=ALU.add),
                     reads=[B_m1, B_st], pwrites=[B_actT])
            gemm("FM", yaT, B_ya, 16, [[("wba", j * 512 + a * 128) for a in range(4)]], epA)
            gemm("FM", ybT3, B_yb3, 8, [[("wbb", j * 512 + a * 128) for a in range(4)]], epB)

        def ep_out(slab, banks):
            c0 = slab[1]
            si = rr["stg"]
            rr["stg"] ^= 1
            st, B_st = stg[si], B_stg[si]
            stv = st[:, :].rearrange("p (s c) -> p s c", c=512)
            P.op("sp", lambda e: e.dma_start(out=stv, in_=H1[row0:row0 + 512, c0:c0 + 512].rearrange(
                "(s p) c -> p s c", p=128)), reads=[B_H1[t]], writes=[B_st], dma=1, chan_buf=B_st)
            P.op("dve", lambda e: e.tensor_tensor(out=st[:, :], in0=ps4(banks), in1=st[:, :], op=ALU.add),
                 reads=[bankb[b] for b in banks], writes=[B_st])
            P.op("sp", lambda e: e.dma_start(out=H2[row0:row0 + 512, c0:c0 + 512].rearrange(
                "(s p) c -> p s c", p=128), in_=stv), reads=[B_st], pwrites=[B_H2[t]], dma=1, chan_buf=B_st)
        gemm("TM", actT, B_actT, KC_D, [("wo", c) for c in range(0, D, 512)], ep_out)

        def end(slot, B_slot):
            P.alias(B_hT, allb)
        add_step(end)
        ffn_tile(H2, row0, B_H2[t], "g2", "w2g", "w2u", "w2d", H3, row0, B_H3[t])
        norm_tile(H3, row0, B_H3[t], "gf", final_out=out_d)

    ntile1 = dbg.get("ntile1", 8)
    for t in range(ntile1):
        ffn_tile(x_d, t * TT, B_x, "g1", "w1g", "w1u", "w1d", H1, t * TT, B_H1[t], fu=(t == 0))
        if dbg.get("stop") == "ffn1":
            continue
        inproj_tile(t)
        if t == 0:
            add_step(lambda slot, B_slot: bg_state.__setitem__("on", True))
    if dbg.get("stop") not in ("ffn1", "inproj"):
        def _attn_step(slot, B_slot):
            assert not bg_units and bg_state["pend"] is None or not bg_units, len(bg_units)
            if bg_state["pend"] is not None:
                bg_state["pend"]()
                bg_state["pend"] = None
            attention()
        add_step(_attn_step)
        if dbg.get("stop") != "attn":
            for t in range(4):
                phase3_tile(t)
    run_steps()
    fin_bufs = [B_out] + [b for b in (B_QA, B_KA, B_VA, B_QB, B_KB, B_VB, B_SG, B_YA, B_YB) if b.w] + \
        [b for b in B_H1 + B_H2 + B_H3 if b.w]
    P.op("sp", lambda e: e.wait_ge(fin_sem, 0), reads=fin_bufs)
    P.finalize()

    fin_sem = E(nc.semaphore("fin"))
    esem = {k: E(nc.semaphore("e_" + k)) for k in P.ops}
    csems = [E(nc.semaphore(f"c{i}")) for i in range(P.nchan)]
    block = E(nc.Block())

    @block.tensor
    def _(e):
        P.replay("pe", e, esem, csems)

    @block.scalar
    def _(e):
        P.replay("act", e, esem, csems)

    @block.vector
    def _(e):
        P.replay("dve", e, esem, csems)

    @block.gpsimd
    def _(e):
        P.replay("pool", e, esem, csems)

    @block.sync
    def _(e):
        P.replay("sp", e, esem, csems)

    es.close()
    return nc


def _consts():
    pos = np.arange(T)
    inv = (10000.0 ** (-np.arange(0, 64, 2, dtype=np.float32) / 64.0)).astype(np.float32)
    ang_r = (pos // 64).astype(np.float32)[:, None] * inv[None, :]
    ang_c = (pos % 64).astype(np.float32)[:, None] * inv[None, :]
    ang = np.concatenate([ang_r, ang_c], axis=1)
    cos = np.tile(np.cos(ang).astype(np.float32), (1, 4))
    sin = np.tile(np.sin(ang).astype(np.float32), (1, 4))
    slopes = np.exp2(-8.0 * np.arange(1, BH + 1, dtype=np.float32) / BH).astype(np.float32)
    kk = np.arange(128)[:, None, None]
    jj = np.arange(2)[None, :, None]
    qq = np.arange(128)[None, None, :]
    delta = kk - 64 - qq + 128 * jj
    valid = np.abs(delta) <= 64
    et = np.zeros((BH, 128, 2, 128), np.float32)
    for h in range(BH):
        Dd = DIL[h // 8]
        et[h] = np.where(valid, np.exp(-slopes[h] * np.float32(Dd) * np.abs(delta).astype(np.float32)), 0.0)
    return cos, sin, et.reshape(BH, 128, 256), np.eye(128, dtype=np.float32).astype(ml_dtypes.bfloat16)


def _in_maps(inputs, cores):
    cos, sin, et, idn = _consts()
    f = lambda a: np.ascontiguousarray(np.asarray(a, dtype=np.float32))
    shared = {
        "w1g": f(inputs["w1_gate"][0]), "w1u": f(inputs["w1_up"][0]), "w1d": f(inputs["w1_down"][0]),
        "win": f(inputs["w_in"][0]), "wba": f(inputs["w_branch_a"][0]), "wbb": f(inputs["w_branch_b"][0]),
        "wo": f(inputs["w_out"][0]), "w2g": f(inputs["w2_gate"][0]), "w2u": f(inputs["w2_up"][0]),
        "w2d": f(inputs["w2_down"][0]),
        "g1": f(inputs["g_ffn1"]).reshape(1, D), "gm": f(inputs["g_mix"]).reshape(1, D),
        "g2": f(inputs["g_ffn2"]).reshape(1, D), "gf": f(inputs["g_final"]).reshape(1, D),
        "qn": f(inputs["q_norm_a"]).reshape(1, HD), "kn": f(inputs["k_norm_a"]).reshape(1, HD),
        "etab": et, "idn": idn,
    }
    x = np.asarray(inputs["x"], dtype=np.float32)
    maps = []
    for c in cores:
        b, half = c // 2, c % 2
        m = dict(shared)
        if half == 0:
            m["x"] = np.ascontiguousarray(x[b])
            m["cosr"], m["sinr"] = cos, sin
        else:
            m["x"] = np.ascontiguousarray(x[b, ::-1])
            m["cosr"], m["sinr"] = np.ascontiguousarray(cos[::-1]), np.ascontiguousarray(sin[::-1])
        maps.append(m)
    return maps


def kernel(**inputs):
    nc = build()
    cores = list(range(8))
    res = run_bass_kernel_spmd(nc, _in_maps(inputs, cores), core_ids=cores)
    out = np.empty((NB, T, D), np.float32)
    for c in cores:
        b, half = c // 2, c % 2
        o = res.results[c]["out"]
        if half == 0:
            out[b, 0:OWN] = o
        else:
            out[b, OWN:T] = o[::-1]
    return out
```
